# Optimizing a Trainium2 kernel written in Bass

```python
import math
import jax, jax.numpy as jnp
from jax import lax
import numpy as np

D_MODEL = 1024
BATCH = 8
SEQ = 2048
DEPTH = 2

CHUNK = 64
Q_BLOCK = 128
N_A = DEPTH // 2
N_B = DEPTH - N_A
NORM_EPS = 1e-6

MLA_HEADS = D_MODEL // 128
MLA_NOPE = 128
MLA_ROPE = 64
MLA_V = 128
MLA_Q_LORA = (3 * D_MODEL) // 4
MLA_KV_LORA = D_MODEL // 2
MLA_THETA = 10000.0
MLA_GATE = MLA_HEADS * MLA_V
MLA_IN = MLA_Q_LORA + MLA_KV_LORA + MLA_ROPE + MLA_GATE

DIFF_HEADS = D_MODEL // 128
DIFF_QK = 64
DIFF_V = 2 * DIFF_QK
DIFF_ROT = DIFF_QK // 4
DIFF_THETA = 500000.0
DIFF_Q_WIDTH = DIFF_HEADS * 2 * DIFF_QK
DIFF_GATE = DIFF_HEADS * DIFF_V
DIFF_IN = DIFF_Q_WIDTH + DIFF_GATE
DIFF_KV = DIFF_HEADS * 2 * DIFF_QK + DIFF_HEADS * DIFF_V

kernel_name = "yoco_mla_diffattn_streaming_hybrid"


def rmsnorm(x, g):
    xf = x.astype(jnp.float32)
    y = xf * lax.rsqrt(jnp.mean(xf * xf, axis=-1, keepdims=True) + NORM_EPS)
    return (y * g.astype(jnp.float32)).astype(x.dtype)


def rope_tables(positions, rot_dim, theta):
    inv = theta ** (-jnp.arange(0, rot_dim, 2, dtype=jnp.float32) / rot_dim)
    ang = positions.astype(jnp.float32)[..., None] * inv
    return jnp.cos(ang), jnp.sin(ang)


def apply_rope(x, cos, sin):
    r = cos.shape[-1]
    cos = cos.astype(x.dtype)
    sin = sin.astype(x.dtype)
    x1, x2, rest = x[..., :r], x[..., r:2 * r], x[..., 2 * r:]
    return jnp.concatenate([x1 * cos - x2 * sin, x2 * cos + x1 * sin, rest], axis=-1)


def chunk_mask(blk, seq_len):
    qpos = blk * Q_BLOCK + jnp.arange(Q_BLOCK)
    kpos = jnp.arange(seq_len)
    return (kpos[None, :] // CHUNK) <= (qpos[:, None] // CHUNK)


def masked_softmax(s, mask):
    return jax.nn.softmax(jnp.where(mask[None, None], s, -jnp.inf), axis=-1)


def sweep_query_blocks(fn, seq_len, *qs):
    nb = seq_len // Q_BLOCK
    blocks = tuple(jnp.moveaxis(a.reshape(a.shape[0], nb, Q_BLOCK, *a.shape[2:]), 1, 0) for a in qs)
    out = lax.map(lambda args: fn(args[0], *args[1]), (jnp.arange(nb), blocks))
    out = jnp.moveaxis(out, 0, 1)
    return out.reshape(out.shape[0], seq_len, *out.shape[3:])


def mla_layer(h, cos, sin, pre_g, w_in, q_norm_g, w_uq, kv_norm_g, w_uk, w_uv, w_o, post_g):
    B, S, _ = h.shape
    u = rmsnorm(h, pre_g)
    proj = u @ w_in
    c_q, c_kv, k_r, z = jnp.split(
        proj, [MLA_Q_LORA, MLA_Q_LORA + MLA_KV_LORA, MLA_Q_LORA + MLA_KV_LORA + MLA_ROPE], axis=-1)
    q = (rmsnorm(c_q, q_norm_g) @ w_uq).reshape(B, S, MLA_HEADS, MLA_NOPE + MLA_ROPE)
    q_n = q[..., :MLA_NOPE]
    q_r = apply_rope(q[..., MLA_NOPE:], cos[:, :, None], sin[:, :, None])
    c_kv = rmsnorm(c_kv, kv_norm_g)
    k_n = (c_kv @ w_uk).reshape(B, S, MLA_HEADS, MLA_NOPE)
    v = (c_kv @ w_uv).reshape(B, S, MLA_HEADS, MLA_V)
    k_r = apply_rope(k_r, cos, sin)
    scale = (MLA_NOPE + MLA_ROPE) ** -0.5

    def attend(blk, qn, qr):
        s = (jnp.einsum('bqhd,bkhd->bhqk', qn, k_n)
             + jnp.einsum('bqhr,bkr->bhqk', qr, k_r)).astype(jnp.float32) * scale
        p = masked_softmax(s, chunk_mask(blk, S)).astype(v.dtype)
        return jnp.einsum('bhqk,bkhe->bqhe', p, v)

    o = sweep_query_blocks(attend, S, q_n, q_r).reshape(B, S, MLA_HEADS * MLA_V)
    o = (o * jax.nn.silu(z)) @ w_o
    return h + rmsnorm(o, post_g)


def shared_kv(h, cos, sin, kv_norm_g, w_kv):
    B, S, _ = h.shape
    u = rmsnorm(h, kv_norm_g)
    k, v = jnp.split(u @ w_kv, [DIFF_HEADS * 2 * DIFF_QK], axis=-1)
    k = apply_rope(k.reshape(B, S, DIFF_HEADS, 2, DIFF_QK), cos[:, :, None, None], sin[:, :, None, None])
    return k[..., 0, :], k[..., 1, :], v.reshape(B, S, DIFF_HEADS, DIFF_V)


def diff_layer(h, layer_idx, cos, sin, k1, k2, v, pre_g, w_in, lam, subln_g, w_o, post_g):
    B, S, _ = h.shape
    u = rmsnorm(h, pre_g)
    q, z = jnp.split(u @ w_in, [DIFF_Q_WIDTH], axis=-1)
    q = apply_rope(q.reshape(B, S, DIFF_HEADS, 2, DIFF_QK), cos[:, :, None, None], sin[:, :, None, None])
    q1, q2 = q[..., 0, :], q[..., 1, :]
    lam_init = 0.8 - 0.6 * math.exp(-0.3 * layer_idx)
    lf = lam.astype(jnp.float32)
    lam_full = jnp.exp(jnp.sum(lf[0] * lf[1])) - jnp.exp(jnp.sum(lf[2] * lf[3])) + lam_init
    scale = DIFF_QK ** -0.5

    def attend(blk, qa, qb):
        mask = chunk_mask(blk, S)
        s1 = jnp.einsum('bqhd,bkhd->bhqk', qa, k1).astype(jnp.float32) * scale
        s2 = jnp.einsum('bqhd,bkhd->bhqk', qb, k2).astype(jnp.float32) * scale
        p = masked_softmax(s1, mask) - lam_full * masked_softmax(s2, mask)
        return jnp.einsum('bhqk,bkhe->bqhe', p.astype(v.dtype), v)

    o = sweep_query_blocks(attend, S, q1, q2)
    o = rmsnorm(o, subln_g) * (1.0 - lam_init)
    o = (o.reshape(B, S, DIFF_HEADS * DIFF_V) * jax.nn.silu(z)) @ w_o
    return h + rmsnorm(o, post_g)


def setup_inputs(seed: int = 0) -> dict:
    key = jax.random.key(seed)
    ks = jax.random.split(key, 24)
    f32 = jnp.float32

    def w(k, shape, fan_in):
        return jax.random.normal(k, shape, f32) * (fan_in ** -0.5)

    def gain(k, shape):
        return 1.0 + 0.02 * jax.random.normal(k, shape, f32)

    x = jax.random.normal(ks[0], (BATCH, SEQ, D_MODEL), f32)
    start = jax.random.randint(ks[1], (BATCH, 1), 0, 4096, dtype=jnp.int32)
    positions = start + jnp.arange(SEQ, dtype=jnp.int32)[None, :]
    return {
        "x": x,
        "positions": positions,
        "a_pre_g": gain(ks[2], (N_A, D_MODEL)),
        "a_w_in": w(ks[3], (N_A, D_MODEL, MLA_IN), D_MODEL),
        "a_q_norm_g": gain(ks[4], (N_A, MLA_Q_LORA)),
        "a_w_uq": w(ks[5], (N_A, MLA_Q_LORA, MLA_HEADS * (MLA_NOPE + MLA_ROPE)), MLA_Q_LORA),
        "a_kv_norm_g": gain(ks[6], (N_A, MLA_KV_LORA)),
        "a_w_uk": w(ks[7], (N_A, MLA_KV_LORA, MLA_HEADS * MLA_NOPE), MLA_KV_LORA),
        "a_w_uv": w(ks[8], (N_A, MLA_KV_LORA, MLA_HEADS * MLA_V), MLA_KV_LORA),
        "a_w_o": w(ks[9], (N_A, MLA_HEADS * MLA_V, D_MODEL), MLA_HEADS * MLA_V),
        "a_post_g": gain(ks[10], (N_A, D_MODEL)),
        "b_kv_norm_g": gain(ks[11], (D_MODEL,)),
        "b_w_kv": w(ks[12], (D_MODEL, DIFF_KV), D_MODEL),
        "b_pre_g": gain(ks[13], (N_B, D_MODEL)),
        "b_w_in": w(ks[14], (N_B, D_MODEL, DIFF_IN), D_MODEL),
        "b_lambda": 0.1 * jax.random.normal(ks[15], (N_B, 4, DIFF_QK), f32),
        "b_subln_g": gain(ks[16], (N_B, DIFF_V)),
        "b_w_o": w(ks[17], (N_B, DIFF_HEADS * DIFF_V, D_MODEL), DIFF_HEADS * DIFF_V),
        "b_post_g": gain(ks[18], (N_B, D_MODEL)),
    }


def reference(x, positions, a_pre_g, a_w_in, a_q_norm_g, a_w_uq, a_kv_norm_g, a_w_uk, a_w_uv,
              a_w_o, a_post_g, b_kv_norm_g, b_w_kv, b_pre_g, b_w_in, b_lambda, b_subln_g,
              b_w_o, b_post_g):
    cos_a, sin_a = rope_tables(positions, MLA_ROPE, MLA_THETA)
    cos_b, sin_b = rope_tables(positions, DIFF_ROT, DIFF_THETA)
    h = x
    k1 = k2 = v = None
    for layer in range(DEPTH):
        if layer < N_A:
            h = mla_layer(h, cos_a, sin_a, a_pre_g[layer], a_w_in[layer], a_q_norm_g[layer],
                          a_w_uq[layer], a_kv_norm_g[layer], a_w_uk[layer], a_w_uv[layer],
                          a_w_o[layer], a_post_g[layer])
        else:
            if layer == N_A:
                k1, k2, v = shared_kv(h, cos_b, sin_b, b_kv_norm_g, b_w_kv)
            j = layer - N_A
            h = diff_layer(h, layer, cos_b, sin_b, k1, k2, v, b_pre_g[j], b_w_in[j], b_lambda[j],
                           b_subln_g[j], b_w_o[j], b_post_g[j])
    return h
```

```python
import math
import numpy as np
import concourse.bass as bass
import concourse.mybir as mybir
from concourse.bass_utils import run_bass_kernel_spmd
from contextlib import ExitStack

F32 = mybir.dt.float32
BF16 = mybir.dt.bfloat16
I32 = mybir.dt.int32
AF = mybir.ActivationFunctionType
ALU = mybir.AluOpType

PE, ACT, DVE, POOL, SP = "tensor", "scalar", "vector", "gpsimd", "sync"
ENGINES = (PE, ACT, DVE, POOL, SP)

S = 2048
D = 1024
NT = 16
EPS = 1e-6
LAM_INIT = 0.8 - 0.6 * math.exp(-0.3 * 1)


class Buf:
    __slots__ = ("name", "last_w", "readers", "excl")

    def __init__(self, name):
        self.name = name
        self.last_w = None
        self.readers = []
        self.excl = False


class Op:
    __slots__ = ("eng", "fn", "deps", "is_dma", "sem_key", "count", "signal")

    def __init__(self, eng, fn, is_dma=False, sem_key=None):
        self.eng = eng
        self.fn = fn
        self.deps = []
        self.is_dma = is_dma
        self.sem_key = sem_key
        self.count = None
        self.signal = False


class Prog:
    def __init__(self, nc):
        self.nc = nc
        self.ops = []
        self.es = ExitStack()
        self.bufs = {}
        self.dummy = None

    def B(self, name):
        b = self.bufs.get(name)
        if b is None:
            b = self.bufs[name] = Buf(name)
        return b

    def sb(self, name, shape, dt):
        return self.es.enter_context(self.nc.sbuf_tensor(name, list(shape), dt))

    def ps(self, name, shape, dt):
        return self.es.enter_context(self.nc.psum_tensor(name, list(shape), dt))

    def op(self, eng, meth, kw=None, reads=(), writes=(), dma_key=None):
        if isinstance(meth, str):
            fn = (lambda e, meth=meth, kw=kw: getattr(e, meth)(**kw))
        else:
            fn = meth
        o = Op(eng, fn, is_dma=dma_key is not None, sem_key=dma_key)
        seen = set()

        def add(d, kind):
            if d is None or d is o:
                return
            k = (id(d), kind)
            if k in seen:
                return
            seen.add(k)
            o.deps.append((d, kind))

        reads = [self.B(b) if isinstance(b, str) else b for b in reads]
        writes = [self.B(b) if isinstance(b, str) else b for b in writes]
        for b in reads:
            add(b.last_w, "RAW")
            if b.excl:
                for r in b.readers:
                    if r.eng != eng:
                        add(r, "WAR")
        for b in writes:
            add(b.last_w, "WAW")
            for r in b.readers:
                add(r, "WAR")
        for b in reads:
            b.readers.append(o)
        for b in writes:
            b.last_w = o
            b.readers = []
        self.ops.append(o)
        if isinstance(kw, dict) and kw.get("accum_out") is not None and self.dummy is not None:
            if eng == ACT:
                return self.op(ACT, "copy", dict(out=self.dummy[1], in_=self.dummy[0]), reads=["eps"], writes=writes)
            if eng == DVE:
                return self.op(DVE, "tensor_copy", dict(out=self.dummy[3], in_=self.dummy[2]), reads=["eps"], writes=writes)
        return o

    def dma(self, out, in_, reads=(), writes=(), key=None, **kw):
        return self.op(SP, "dma_start", dict(out=out, in_=in_, **kw), reads, writes, dma_key=key)

    def emit(self, final_wait_keys=()):
        nc = self.nc
        for o in self.ops:
            for d, kind in o.deps:
                if d.is_dma:
                    continue
                if d.eng == o.eng and (d.eng == PE or kind != "RAW"):
                    continue
                d.signal = True
        eng_cnt = {e: 0 for e in ENGINES}
        key_cnt = {}
        for o in self.ops:
            if o.is_dma:
                key_cnt[o.sem_key] = key_cnt.get(o.sem_key, 0) + 16
                o.count = key_cnt[o.sem_key]
            elif o.signal:
                eng_cnt[o.eng] += 1
                o.count = eng_cnt[o.eng]
        sems = {}
        for e in (PE, ACT, DVE, POOL):
            sems[e] = self.es.enter_context(nc.semaphore("s_" + e))
        for k in key_cnt:
            sems[("dma", k)] = self.es.enter_context(nc.semaphore("d_%s" % (k,)))
        per_eng = {e: [o for o in self.ops if o.eng == e] for e in ENGINES}
        final = [(sems[("dma", k)], key_cnt[k]) for k in final_wait_keys]

        def run(e, handle):
            waited = {}
            for o in per_eng[e]:
                for d, kind in o.deps:
                    if d.is_dma:
                        kk = ("dma", d.sem_key)
                    else:
                        if d.eng == e and (e == PE or kind != "RAW"):
                            continue
                        kk = d.eng
                    if waited.get(kk, 0) >= d.count:
                        continue
                    waited[kk] = d.count
                    handle.wait_ge(sems[kk], d.count)
                ins = o.fn(handle)
                if o.is_dma:
                    ins.then_inc(sems[("dma", o.sem_key)], 16)
                elif o.signal:
                    ins.then_inc(sems[e], 1)
            if e == SP:
                for s, c in final:
                    handle.wait_ge(s, c)

        with nc.Block() as block:
            @block.tensor
            def _(h):
                run(PE, h)

            @block.scalar
            def _(h):
                run(ACT, h)

            @block.vector
            def _(h):
                run(DVE, h)

            @block.gpsimd
            def _(h):
                run(POOL, h)

            @block.sync
            def _(h):
                run(SP, h)
        self.es.close()


def build(n_layers=2, debug=False, nheads=8):
    nc = bass.Bass("TRN2", target_bir_lowering=False)

    def din(name, shape, dt=F32):
        return nc.dram_tensor(name, list(shape), dt, kind="ExternalInput").ap()

    x = din("x", [S, D])
    pos = din("pos", [128, NT], I32)
    a_w_in = din("a_w_in", [D, 2368])
    a_w_uq = din("a_w_uq", [768, 1536])
    a_w_uk = din("a_w_uk", [512, 1024])
    a_w_uv = din("a_w_uv", [512, 1024])
    a_w_o = din("a_w_o", [D, D])
    b_w_kv = din("b_w_kv", [D, 2048])
    b_w_in = din("b_w_in", [D, 2048])
    b_w_o = din("b_w_o", [D, D])
    gains = din("gains", [128, 34])
    gpost_a = din("gpost_a", [128, D])
    gpost_b = din("gpost_b", [128, D])
    lam_in = din("lam_in", [128, 256])
    subln_in = din("subln_in", [128, 128])
    cst = din("cst", [128, 40])
    ident_in = din("ident_in", [128, 128])
    out = nc.dram_tensor("out", [S, D], F32, kind="ExternalOutput").ap()
    zscr = nc.dram_tensor("zscr", [S, D], F32, kind="Internal").ap()

    P = Prog(nc)
    op = P.op

    h = P.sb("h", [128, NT, D], F32)
    arA = P.sb("arA", [128, 22528], BF16)
    arB = P.sb("arB", [128, 19456], BF16)
    arC = P.sb("arC", [128, 16384], BF16)
    stg = [P.sb("stg%d" % i, [128, 1184], F32) for i in range(2)]
    ident = P.sb("ident", [128, 128], BF16)
    cosA = P.sb("cosA", [128, NT, 32], F32)
    sinA = P.sb("sinA", [128, NT, 32], F32)
    cosB = P.sb("cosB", [128, NT, 8], F32)
    sinB = P.sb("sinB", [128, NT, 8], F32)
    gpost = P.sb("gpost", [128, D], F32)
    ET = [P.sb("ET%d" % i, [128, 512], BF16) for i in range(4)]
    o1n = P.sb("o1n", [128, 4, 128], F32)
    rtmp = P.sb("rtmp", [128, 256], F32)
    kqb = P.sb("kqb", [128, 4, 256], BF16)
    gn = P.sb("gn", [128, 34], F32)
    junkd = P.sb("junkd", [128, 128], BF16)
    mkq = P.sb("mkq", [1, 256], BF16)
    g2b = P.sb("g2b", [128, 128], F32)
    st = P.sb("st", [128, 64], F32)
    RSTDH = 0
    SSH = 16
    SSQ = 32
    LNT = 34
    RSQ = 38
    RIN = 40
    SSD = 44
    RSD = 48
    NLAM = 52
    EPSC = 53
    LT = 54
    S2 = 58

    def view(arena, boff, shape, dt):
        n = 1
        for s_ in shape[1:]:
            n *= s_
        ne = n * (2 if dt == F32 else 1)
        a = arena[:, boff // 2: boff // 2 + ne]
        if dt == F32:
            a = a.bitcast(F32)
        if len(shape) == 3:
            a = a.rearrange("p (a b) -> p a b", a=shape[1])
        return a

    cqnT = view(arA, 0, [128, 6, S], BF16)
    ckvnT = view(arA, 24576, [128, 4, S], BF16)
    krT = view(arA, 40960, [128, S], BF16)
    uT = view(arA, 0, [128, 8, S], BF16)
    w_in = view(arB, 0, [128, 8, 2368], BF16)
    KT = view(arB, 0, [128, S], BF16)
    QT = view(arB, 4096, [128, S], BF16)
    QrT = view(arB, 8192, [128, S], BF16)
    V = view(arB, 12288, [128, NT, 129], BF16)
    zbuf = view(arB, 16448, [128, NT, 128], F32)
    wq = view(arB, 24640, [128, 6, 192], BF16)
    wk = view(arB, 24640 + 2304, [128, 4, 128], BF16)
    wv = view(arB, 24640 + 3328, [128, 4, 128], BF16)
    wg = view(arB, 24640, [128, 8, 512], BF16)
    w_o = view(arA, 0, [128, 8, D], BF16)
    tmpr = view(arB, 16384, [128, D], F32)
    junk4 = view(arB, 20480, [128, D], BF16)
    aT = [view(arB, 22528 + i * 2048, [128, 8, 128], BF16) for i in range(2)]
    abuf = view(arC, 0, [128, NT, D], BF16)
    u_bf = [view(arC, i * 2048, [128, D], BF16) for i in range(2)]
    uTs = [view(arC, 4096 + i * 2048, [128, 8, 128], BF16) for i in range(2)]
    cq_bf = view(arC, 8192, [128, 768], BF16)
    ckv_bf = view(arC, 9728, [128, 512], BF16)
    kr_bf = view(arC, 10752, [128, 64], BF16)
    junk1 = view(arC, 11264, [128, D], BF16)
    zstage = [view(arC, 13312 + i * 4096, [128, D], F32) for i in range(2)]
    t_posf = view(arC, 0, [128, NT], F32)
    t_ang = view(arC, 1024, [128, NT * 40], F32)
    t_y = view(arC, 4096, [128, NT * 40], F32)
    t_k = view(arC, 7168, [128, NT * 40], F32)
    t_r = view(arC, 10240, [128, NT * 40], F32)
    t_m = view(arC, 13312, [128, NT * 40], F32)
    t_sc = view(arC, 16384, [128, NT, 40], F32)
    t_lam = view(arC, 20480, [128, 256], F32)
    t_sub = view(arC, 22528, [128, 128], F32)
    t_id = view(arC, 24576, [128, 128], F32)
    t_pos = arC[:, 13000:13000 + 2 * NT].bitcast(I32)
    t_ki = view(arC, 28672, [128, NT * 40], F32).bitcast(I32)

    PS = P.ps("PS", [128, 8, 512], F32)
    PSf = PS[:].rearrange("p b n -> p (b n)")
    P.dummy = (st[:, EPSC:EPSC + 1], st[:, 63:64], st[:, EPSC:EPSC + 1], st[:, 61:62])
    PSB = [P.B("psb%d" % i) for i in range(8)]
    for b_ in PSB:
        b_.excl = True

    def psf(i):
        return PS[:, i, :]

    def psb(i):
        return PS[:, i, :].bitcast(BF16)

    P.dma(gn[:], gains[:, :], writes=["gn"], key="c0")
    P.dma(t_pos, pos[:, :], writes=["t_pos"], key="c1")
    P.dma(t_sc[:, 0, :], cst[:, :], writes=["t_sc0"], key="c2")
    P.dma(t_lam, lam_in[:, :], writes=["t_lam"], key="c3")
    P.dma(t_sub, subln_in[:, :], writes=["t_sub"], key="c4")
    P.dma(t_id, ident_in[:, :], writes=["t_id"], key="c5")

    op(DVE, "tensor_copy", dict(out=ident[:], in_=t_id), reads=["t_id"], writes=["ident"])
    op(DVE, "memset", dict(ap=st[:, EPSC:EPSC + 1], constant=EPS), writes=["eps"])
    op(DVE, "memset", dict(ap=mkq[0:1, 0:64], constant=0.0), writes=["mkq"])
    op(DVE, "memset", dict(ap=mkq[0:1, 64:128], constant=1.0), writes=["mkq"])
    op(DVE, "memset", dict(ap=mkq[0:1, 128:192], constant=-30000.0), writes=["mkq"])
    op(DVE, "memset", dict(ap=mkq[0:1, 192:256], constant=0.0), writes=["mkq"])
    op(DVE, "tensor_copy", dict(out=t_posf, in_=t_pos), reads=["t_pos"], writes=["t_posf"])
    op(DVE, "tensor_copy", dict(out=rtmp[:, 0:40], in_=t_sc[:, 0, :]), reads=["t_sc0"], writes=["inv"])
    ang3 = t_ang.rearrange("p (t f) -> p t f", t=NT)
    for tt in range(NT):
        op(DVE, "tensor_scalar", dict(out=ang3[:, tt, :], in0=rtmp[:, 0:40], scalar1=t_posf[:, tt:tt + 1],
                                                 scalar2=None, op0=ALU.mult),
           reads=["inv", "t_posf"], writes=["ang"])
    TWO_PI = 2.0 * math.pi
    C1 = 6.28125
    C2 = TWO_PI - C1
    PI_IN = 3.1415925

    def trig(shift, dst_tok):
        op(DVE, "tensor_scalar", dict(out=t_y, in0=t_ang, scalar1=float(shift), scalar2=None, op0=ALU.add),
           reads=["ang"], writes=["t_y"])
        op(DVE, "tensor_scalar", dict(out=t_ki, in0=t_y, scalar1=float(1.0 / TWO_PI), scalar2=None, op0=ALU.mult),
           reads=["t_y"], writes=["t_ki"])
        op(DVE, "tensor_copy", dict(out=t_k, in_=t_ki), reads=["t_ki"], writes=["t_k"])
        op(DVE, "scalar_tensor_tensor", dict(out=t_r, in0=t_k, scalar=float(-C1), in1=t_y, op0=ALU.mult, op1=ALU.add),
           reads=["t_k", "t_y"], writes=["t_r"])
        op(DVE, "scalar_tensor_tensor", dict(out=t_y, in0=t_k, scalar=float(-C2), in1=t_r, op0=ALU.mult, op1=ALU.add),
           reads=["t_k", "t_r"], writes=["t_y"])
        op(DVE, "tensor_scalar", dict(out=t_m, in0=t_y, scalar1=0.0, scalar2=None, op0=ALU.is_lt),
           reads=["t_y"], writes=["t_m"])
        op(DVE, "scalar_tensor_tensor", dict(out=t_r, in0=t_m, scalar=float(TWO_PI), in1=t_y, op0=ALU.mult, op1=ALU.add),
           reads=["t_m", "t_y"], writes=["t_r"])
        op(DVE, "tensor_scalar", dict(out=t_m, in0=t_r, scalar1=float(TWO_PI), scalar2=None, op0=ALU.is_ge),
           reads=["t_r"], writes=["t_m"])
        op(DVE, "scalar_tensor_tensor", dict(out=t_y, in0=t_m, scalar=float(-TWO_PI), in1=t_r, op0=ALU.mult, op1=ALU.add),
           reads=["t_m", "t_r"], writes=["t_y"])
        op(DVE, "tensor_scalar", dict(out=t_r, in0=t_y, scalar1=float(-math.pi), scalar2=float(-PI_IN),
                                          op0=ALU.add, op1=ALU.max),
           reads=["t_y"], writes=["t_r"])
        op(DVE, "tensor_scalar", dict(out=t_y, in0=t_r, scalar1=float(PI_IN), scalar2=None, op0=ALU.min),
           reads=["t_r"], writes=["t_y"])
        op(ACT, "activation", dict(out=t_sc[:].rearrange("p t f -> p (t f)"), in_=t_y, func=AF.Sin),
           reads=["t_y"], writes=["t_sc"])

    trig(math.pi, None)
    op(DVE, "tensor_copy", dict(out=sinA[:], in_=t_sc[:, :, 0:32]), reads=["t_sc"], writes=["sinA"])
    op(DVE, "tensor_copy", dict(out=sinB[:], in_=t_sc[:, :, 32:40]), reads=["t_sc"], writes=["sinB"])
    trig(1.5 * math.pi, None)
    op(DVE, "tensor_copy", dict(out=cosA[:], in_=t_sc[:, :, 0:32]), reads=["t_sc"], writes=["cosA"])
    op(DVE, "tensor_copy", dict(out=cosB[:], in_=t_sc[:, :, 32:40]), reads=["t_sc"], writes=["cosB"])

    for q_ in range(2):
        op(DVE, "tensor_tensor", dict(out=rtmp[:, 64:128], in0=t_lam[:, q_ * 128:q_ * 128 + 64],
                                      in1=t_lam[:, q_ * 128 + 64:q_ * 128 + 128], op=ALU.mult),
           reads=["t_lam"], writes=["rt64"])
        op(DVE, "tensor_scalar", dict(out=rtmp[:, 128:192], in0=rtmp[:, 64:128], scalar1=1.0, scalar2=None, op0=ALU.mult,
                                      op1=ALU.add, accum_out=st[:, LT + q_:LT + q_ + 1]),
           reads=["rt64"], writes=["lt%d" % q_, "rt128"])
    op(ACT, "activation", dict(out=st[:, LT + 2:LT + 4], in_=st[:, LT:LT + 2], func=AF.Exp),
       reads=["lt0", "lt1"], writes=["lt23"])
    op(DVE, "tensor_tensor", dict(out=st[:, LT:LT + 1], in0=st[:, LT + 3:LT + 4], in1=st[:, LT + 2:LT + 3], op=ALU.subtract),
       reads=["lt23"], writes=["lt0"])
    op(DVE, "tensor_scalar", dict(out=st[:, NLAM:NLAM + 1], in0=st[:, LT:LT + 1], scalar1=float(-LAM_INIT), scalar2=None,
                                      op0=ALU.add),
       reads=["lt0"], writes=["nlam"])
    op(DVE, "tensor_scalar", dict(out=g2b[:], in0=t_sub, scalar1=float(1.0 - LAM_INIT), scalar2=None, op0=ALU.mult),
       reads=["t_sub"], writes=["g2b"])

    rr = {"i": 0, "ev": 0, "stg": 0}
    fence = {"cur": [], "n": 0}

    def nbank(lo=0, hi=8):
        b = lo + rr["i"] % (hi - lo)
        rr["i"] += 1
        return b

    def evac(out_ap, in_ap, reads, writes, eng=None):
        if eng is None:
            eng = ACT if rr["ev"] % 2 == 0 else DVE
            rr["ev"] += 1
        reads = list(reads) + fence["cur"]
        if eng == ACT:
            op(ACT, "copy", dict(out=out_ap, in_=in_ap), reads=reads, writes=writes)
        else:
            op(DVE, "tensor_copy", dict(out=out_ap, in_=in_ap), reads=reads, writes=writes)

    def rstd_calc(ss_ap, n, out_ap, tmp_ap, reads, writes, tmpname):
        op(ACT, "activation", dict(out=tmp_ap, in_=ss_ap, func=AF.Ln, scale=float(1.0 / n), bias=st[:, EPSC:EPSC + 1]),
           reads=list(reads) + ["eps"], writes=[tmpname])
        op(ACT, "activation", dict(out=out_ap, in_=tmp_ap, func=AF.Exp, scale=-0.5),
           reads=[tmpname], writes=writes)

    def pe_fence():
        fence["n"] += 1
        n_ = fence["n"]
        op(PE, "transpose", dict(out=psb(7)[0:32, 0:32], in_=ident[0:32, 0:32], identity=ident[0:32, 0:32]),
           reads=["ident"], writes=[PSB[7], "fP%d" % n_])
        op(DVE, "tensor_copy", dict(out=st[:, 61:62], in_=st[:, EPSC:EPSC + 1]), reads=["eps"], writes=["fD%d" % n_])
        op(ACT, "copy", dict(out=st[:, 63:64], in_=st[:, EPSC:EPSC + 1]), reads=["eps"], writes=["fA%d" % n_])
        op(POOL, "memset", dict(ap=st[:, 59:60], constant=0.0), writes=["fG%d" % n_])
        fence["cur"] = ["fP%d" % n_, "fD%d" % n_, "fA%d" % n_, "fG%d" % n_]

    def load_w(dst, src, shape, gain_ap, dst_tok):
        s = rr["stg"] % 2
        rr["stg"] += 1
        n = 1
        for v_ in shape[1:]:
            n *= v_
        sv = stg[s][:, 0:n]
        if len(shape) == 3:
            sv = sv.rearrange("p (a b) -> p a b", a=shape[1])
        P.dma(sv, src, writes=["stg%d" % s], key="stg%d" % s)
        if gain_ap is None:
            op(POOL, "tensor_copy", dict(out=dst, in_=sv), reads=["stg%d" % s] + fence["cur"],
               writes=dst_tok if isinstance(dst_tok, list) else [dst_tok])
        else:
            op(POOL, "tensor_tensor", dict(out=dst, in0=sv, in1=gain_ap, op=ALU.mult),
               reads=["stg%d" % s, "gn"] + fence["cur"], writes=dst_tok if isinstance(dst_tok, list) else [dst_tok])

    def hsq(layer, tt):
        jk = junk1 if layer == 0 else junk4
        op(ACT, "activation", dict(out=jk, in_=h[:, tt, :], func=AF.Square, accum_out=st[:, SSH + tt:SSH + tt + 1]),
           reads=["h%d" % tt] + fence["cur"], writes=["junk1" if layer == 0 else "junk4", "ssh%d" % (tt // 4)])

    def hrstd(g):
        t0_, t1_ = 4 * g, 4 * g + 4
        rstd_calc(st[:, SSH + t0_:SSH + t1_], D, st[:, RSTDH + t0_:RSTDH + t1_], st[:, RSTDH + t0_:RSTDH + t1_],
                  ["ssh%d" % g], ["rstdh%d" % g], "rstdh_tmp%d" % g)

    def hstats(layer, tiles):
        for tt in tiles:
            hsq(layer, tt)
        hrstd(tiles[0] // 4)

    def attention(hh, layer, nmaps):
        scale = (192 ** -0.5) if layer == 0 else 0.125
        cnt = {"s": 0, "e": 0}
        pending = []
        for qb in range(4):
            nk = 4 * qb + 4
            for m in range(nmaps):
                def s_mm(kt, sbank):
                    c0 = 128 * max(0, kt - 4 * qb)
                    o_ap = psf(sbank)[:, c0:512]
                    diag = kt >= 4 * qb
                    if layer == 0:
                        op(PE, "matmul", dict(out=o_ap, lhsT=KT[:, kt * 128:(kt + 1) * 128],
                                              rhs=QT[:, qb * 512 + c0:(qb + 1) * 512], start=True, stop=False),
                           reads=["KT", "QT"], writes=[PSB[sbank]])
                        op(PE, "matmul", dict(out=o_ap, lhsT=krT[:, kt * 128:(kt + 1) * 128],
                                              rhs=QrT[:, qb * 512 + c0:(qb + 1) * 512], start=False, stop=not diag),
                           reads=["krT", "QrT"], writes=[PSB[sbank]])
                    else:
                        kp = KT if m == 0 else QrT
                        op(PE, "matmul", dict(out=o_ap, lhsT=kp[:, kt * 128:(kt + 1) * 128],
                                              rhs=QT[:, qb * 512 + c0:(qb + 1) * 512], start=True, stop=not diag),
                           reads=["KT", "QrT", "QT"], writes=[PSB[sbank]])
                    if diag:
                        op(PE, "matmul", dict(out=psf(sbank)[:, c0:c0 + 128], lhsT=mkq[0:1, 0:128], rhs=mkq[0:1, 128:256],
                                              start=False, stop=True),
                           reads=["mkq"], writes=[PSB[sbank]])

                def exp_op(kt, sbank, slot):
                    c0 = 128 * max(0, kt - 4 * qb)
                    et = ET[slot]
                    op(ACT, "activation", dict(out=et[:, c0:512], in_=psf(sbank)[:, c0:512], func=AF.Exp,
                                               scale=float(scale)),
                       reads=[PSB[sbank]], writes=["ET%d" % slot])

                def pv_op(kt, slot):
                    et = ET[slot]
                    for qt in range(max(0, kt - 4 * qb), 4):
                        last = (kt == 4 * qb + qt)
                        op(PE, "matmul", dict(out=psf(2 + qt)[:, 0:129], lhsT=et[:, qt * 128:(qt + 1) * 128],
                                              rhs=V[:, kt, :], start=(kt == 0), stop=last),
                           reads=["ET%d" % slot, "V"], writes=[PSB[2 + qt]])
                        if last:
                            epilogue(qt)

                def epilogue(qt):
                    tt = 4 * qb + qt
                    ob = psf(2 + qt)
                    rin = st[:, RIN + qt:RIN + qt + 1]
                    op(DVE, "reciprocal", dict(out=rin, in_=ob[:, 128:129]), reads=[PSB[2 + qt]], writes=["rin%d" % qt])
                    if layer == 0:
                        op(DVE, "scalar_tensor_tensor", dict(out=abuf[:, tt, hh * 128:(hh + 1) * 128], in0=ob[:, 0:128],
                                                                 scalar=rin, in1=zbuf[:, tt, :], op0=ALU.mult, op1=ALU.mult),
                           reads=[PSB[2 + qt], "rin%d" % qt, "zbuf"], writes=["abuf%d" % tt])
                    elif m == 0:
                        op(DVE, "tensor_scalar", dict(out=o1n[:, qt, :], in0=ob[:, 0:128], scalar1=rin, scalar2=None,
                                                          op0=ALU.mult),
                           reads=[PSB[2 + qt], "rin%d" % qt], writes=["o1n%d" % qt])
                    else:
                        s2 = st[:, S2 + qt:S2 + qt + 1]
                        op(DVE, "tensor_tensor", dict(out=s2, in0=rin, in1=st[:, NLAM:NLAM + 1], op=ALU.mult),
                           reads=["rin%d" % qt, "nlam"], writes=["s2%d" % qt])
                        op(DVE, "scalar_tensor_tensor", dict(out=o1n[:, qt, :], in0=ob[:, 0:128], scalar=s2,
                                                                 in1=o1n[:, qt, :], op0=ALU.mult, op1=ALU.add),
                           reads=[PSB[2 + qt], "s2%d" % qt, "o1n%d" % qt], writes=["o1n%d" % qt])
                        op(DVE, "scalar_tensor_tensor", dict(out=rtmp[:, 0:128], in0=o1n[:, qt, :], scalar=1.0,
                                                             in1=o1n[:, qt, :], op0=ALU.mult, op1=ALU.mult,
                                                             accum_out=st[:, SSD + qt:SSD + qt + 1]),
                           reads=["o1n%d" % qt], writes=["ssd", "rt0"])

                SBK = (0, 1, 6)
                base = cnt["s"]
                for kt in range(min(2, nk)):
                    s_mm(kt, SBK[(base + kt) % 3])
                for kt in range(nk):
                    slot = cnt["e"] % 4
                    cnt["e"] += 1
                    exp_op(kt, SBK[(base + kt) % 3], slot)
                    if kt + 2 < nk:
                        s_mm(kt + 2, SBK[(base + kt + 2) % 3])
                    pv_op(kt, slot)
                    if kt == 1 and pending:
                        pending.pop(0)()
                cnt["s"] += nk
                if layer == 1 and m == 1:
                    def finalize(qb=qb):
                        rstd_calc(st[:, SSD:SSD + 4], 128, st[:, RSD:RSD + 4], st[:, LNT:LNT + 4], ["ssd"], ["rsd"], "lnt")
                        for qt in range(4):
                            tt = 4 * qb + qt
                            op(DVE, "scalar_tensor_tensor", dict(
                                out=abuf[:, tt, hh * 128:(hh + 1) * 128], in0=o1n[:, qt, :], scalar=st[:, RSD + qt:RSD + qt + 1],
                                in1=zbuf[:, tt, :], op0=ALU.mult, op1=ALU.mult),
                               reads=["o1n%d" % qt, "rsd", "zbuf"], writes=["abuf%d" % tt])
                    pending.append(finalize)
        while pending:
            pending.pop(0)()

    def load_wo(w_dram, gp_dram, dead_toks):
        P.dma(gpost[:], gp_dram[:, :], writes=["gpost"], key="gp")
        for k in range(8):
            load_w(w_o[:, k, :], w_dram[k * 128:(k + 1) * 128, :], [128, D], None, ["w_o"] + dead_toks)

    def out_proj(layer, w_dram, gp_dram, last):
        def stage_t(tt):
            sl = tt % 2
            tb = tt % 2
            for c in range(8):
                op(PE, "transpose", dict(out=psb(tb)[:, c * 128:(c + 1) * 128], in_=abuf[:, tt, c * 128:(c + 1) * 128],
                                         identity=ident[:]),
                   reads=["abuf%d" % tt, "ident"], writes=[PSB[tb]])
            evac(aT[sl][:].rearrange("p c t -> p (c t)"), psb(tb)[:, 0:1024], [PSB[tb]], ["aT%d" % sl])

        def stage_m(tt):
            sl = tt % 2
            pb = 2 + 2 * (tt % 3)
            for half in range(2):
                for k in range(8):
                    op(PE, "matmul", dict(out=psf(pb + half), lhsT=aT[sl][:, k, :],
                                          rhs=w_o[:, k, half * 512:(half + 1) * 512],
                                          start=(k == 0), stop=(k == 7)),
                       reads=["aT%d" % sl, "w_o"], writes=[PSB[pb + half]])
            pso = PSf[:, pb * 512:pb * 512 + 1024]
            q_ = tt % 2
            op(ACT, "activation", dict(out=junk4, in_=pso, func=AF.Square, accum_out=st[:, SSQ + q_:SSQ + q_ + 1]),
               reads=[PSB[pb], PSB[pb + 1]] + fence["cur"], writes=["junk4", "sso%d" % q_])
            rstd_calc(st[:, SSQ + q_:SSQ + q_ + 1], D, st[:, RSQ + q_:RSQ + q_ + 1], st[:, LNT + q_:LNT + q_ + 1],
                      ["sso%d" % q_], ["rso%d" % q_], "lnt%d" % q_)
            op(DVE, "scalar_tensor_tensor", dict(out=tmpr, in0=pso, scalar=st[:, RSQ + q_:RSQ + q_ + 1], in1=gpost[:],
                                                 op0=ALU.mult, op1=ALU.mult),
               reads=[PSB[pb], PSB[pb + 1], "rso%d" % q_, "gpost"] + fence["cur"], writes=["tmpr"])
            op(POOL, "tensor_tensor", dict(out=h[:, tt, :], in0=h[:, tt, :], in1=tmpr, op=ALU.add),
               reads=["tmpr", "h%d" % tt], writes=["h%d" % tt])
            if last:
                P.dma(out[tt * 128:(tt + 1) * 128, :], h[:, tt, :], reads=["h%d" % tt], key="out")

        def next_stats(t_):
            hsq(1, t_)
            if t_ % 4 == 3:
                hrstd(t_ // 4)

        stage_t(0)
        for tt in range(NT):
            if tt + 1 < NT:
                stage_t(tt + 1)
            stage_m(tt)
            if not last and n_layers == 2 and tt >= 2:
                next_stats(tt - 2)
        if not last and n_layers == 2:
            next_stats(NT - 2)
            next_stats(NT - 1)

    def xload(tiles):
        for tt in tiles:
            P.dma(h[:, tt, :], x[tt * 128:(tt + 1) * 128, :], writes=["h%d" % tt], key="x%d" % tt)

    def w_in_load(ks):
        n_ = 0
        for k in ks:
            for hf in range(2):
                dst = w_in[:, k, hf * 1184:(hf + 1) * 1184]
                srcw = a_w_in[k * 128:(k + 1) * 128, hf * 1184:(hf + 1) * 1184]
                gain = gn[:, k:k + 1].to_broadcast([128, 1184])
                if n_ < 8:
                    sv = view(arA, n_ * 4736, [128, 1184], F32)
                    P.dma(sv, srcw, writes=["stgA%d" % n_], key="stgA%d" % n_)
                    op(POOL, "tensor_tensor", dict(out=dst, in0=sv, in1=gain, op=ALU.mult),
                       reads=["stgA%d" % n_, "gn"], writes=["w_in"])
                else:
                    load_w(dst, srcw, [128, 1184], gain, "w_in")
                n_ += 1

    xload(range(0, 4))
    w_in_load(range(0, 8))
    op(POOL, "memset", dict(ap=krT[64:128, :], constant=0.0), reads=["w_in"], writes=["krT"])
    xload(range(4, 16))
    hstats(0, list(range(0, 4)))

    def p1_a(tt):
        sl = tt % 2
        op(DVE, "tensor_scalar", dict(out=u_bf[sl], in0=h[:, tt, :], scalar1=st[:, RSTDH + tt:RSTDH + tt + 1],
                                      scalar2=None, op0=ALU.mult),
           reads=["h%d" % tt, "rstdh%d" % (tt // 4)], writes=["u_bf%d" % sl])
        for c in range(8):
            op(PE, "transpose", dict(out=psb(0)[:, c * 128:(c + 1) * 128], in_=u_bf[sl][:, c * 128:(c + 1) * 128],
                                     identity=ident[:]),
               reads=["u_bf%d" % sl, "ident"], writes=[PSB[0]])
        evac(uTs[sl][:].rearrange("p c t -> p (c t)"), psb(0)[:, 0:1024], [PSB[0]], ["uTs%d" % sl], eng=ACT)

    def p1_b(tt):
        sl = tt % 2
        blocks = [(1, 0, 0, 512), (2, 0, 512, 768), (2, 256, 1280, 1344), (3, 0, 768, 1280),
                  (5, 0, 1344, 1856), (6, 0, 1856, 2368)]
        for (bk, po, c0, c1) in blocks:
            for k in range(8):
                op(PE, "matmul", dict(out=psf(bk)[:, po:po + c1 - c0], lhsT=uTs[sl][:, k, :],
                                      rhs=w_in[:, k, c0:c1], start=(k == 0), stop=(k == 7)),
                   reads=["uTs%d" % sl, "w_in"], writes=[PSB[bk]])
        cq_ps = PSf[:, 512:512 + 768]
        ckv_ps = psf(3)
        op(ACT, "activation", dict(out=junk1[:, 0:768], in_=cq_ps, func=AF.Square, accum_out=st[:, SSQ:SSQ + 1]),
           reads=[PSB[1], PSB[2]], writes=["junk1", "ssq"])
        op(ACT, "activation", dict(out=junk1[:, 0:512], in_=ckv_ps, func=AF.Square, accum_out=st[:, SSQ + 1:SSQ + 2]),
           reads=[PSB[3]], writes=["junk1", "sskv"])
        rstd_calc(st[:, SSQ:SSQ + 1], 768, st[:, RSQ:RSQ + 1], st[:, LNT:LNT + 1], ["ssq"], ["rsq"], "lnt")
        rstd_calc(st[:, SSQ + 1:SSQ + 2], 512, st[:, RSQ + 1:RSQ + 2], st[:, LNT + 1:LNT + 2], ["sskv"], ["rskv"], "lnt1")
        op(DVE, "tensor_scalar", dict(out=cq_bf, in0=cq_ps, scalar1=st[:, RSQ:RSQ + 1], scalar2=None, op0=ALU.mult),
           reads=[PSB[1], PSB[2], "rsq"], writes=["cq_bf"])
        op(DVE, "tensor_scalar", dict(out=ckv_bf, in0=ckv_ps, scalar1=st[:, RSQ + 1:RSQ + 2], scalar2=None, op0=ALU.mult),
           reads=[PSB[3], "rskv"], writes=["ckv_bf"])
        kr_ps = psf(2)[:, 256:320]
        t1, t2, t3, t4 = rtmp[:, 0:32], rtmp[:, 32:64], rtmp[:, 64:96], rtmp[:, 96:128]
        cs, sn = cosA[:, tt, :], sinA[:, tt, :]
        op(DVE, "tensor_tensor", dict(out=t1, in0=kr_ps[:, 0:32], in1=cs, op=ALU.mult), reads=[PSB[2], "cosA"], writes=["rt0"])
        op(DVE, "tensor_tensor", dict(out=t2, in0=kr_ps[:, 32:64], in1=sn, op=ALU.mult), reads=[PSB[2], "sinA"], writes=["rt1"])
        op(DVE, "tensor_tensor", dict(out=kr_bf[:, 0:32], in0=t1, in1=t2, op=ALU.subtract), reads=["rt0", "rt1"], writes=["kr_bf"])
        op(DVE, "tensor_tensor", dict(out=t3, in0=kr_ps[:, 32:64], in1=cs, op=ALU.mult), reads=[PSB[2], "cosA"], writes=["rt2"])
        op(DVE, "tensor_tensor", dict(out=t4, in0=kr_ps[:, 0:32], in1=sn, op=ALU.mult), reads=[PSB[2], "sinA"], writes=["rt3"])
        op(DVE, "tensor_tensor", dict(out=kr_bf[:, 32:64], in0=t3, in1=t4, op=ALU.add), reads=["rt2", "rt3"], writes=["kr_bf"])
        zs = zstage[sl]
        op(ACT, "activation", dict(out=zs, in_=PSf[:, 5 * 512:7 * 512], func=AF.Silu),
           reads=[PSB[5], PSB[6]], writes=["zstage%d" % sl])
        P.dma(zscr[tt * 128:(tt + 1) * 128, :], zs, reads=["zstage%d" % sl], writes=["zscr%d" % tt], key="zst%d" % sl)

    def p1_c(tt):
        for c in range(6):
            op(PE, "transpose", dict(out=psb(7)[:, c * 128:(c + 1) * 128], in_=cq_bf[:, c * 128:(c + 1) * 128],
                                     identity=ident[:]),
               reads=["cq_bf", "ident"], writes=[PSB[7]])
        evac(cqnT[:, :, tt * 128:(tt + 1) * 128], psb(7)[:, 0:768].rearrange("p (c t) -> p c t", c=6), [PSB[7]], ["cqnT"], eng=DVE)
        for c in range(4):
            op(PE, "transpose", dict(out=psb(4)[:, c * 128:(c + 1) * 128], in_=ckv_bf[:, c * 128:(c + 1) * 128],
                                     identity=ident[:]),
               reads=["ckv_bf", "ident"], writes=[PSB[4]])
        op(PE, "transpose", dict(out=psb(4)[0:64, 512:640], in_=kr_bf, identity=ident[:]),
           reads=["kr_bf", "ident"], writes=[PSB[4]])
        evac(ckvnT[:, :, tt * 128:(tt + 1) * 128], psb(4)[:, 0:512].rearrange("p (c t) -> p c t", c=4), [PSB[4]], ["ckvnT"], eng=ACT)
        evac(krT[0:64, tt * 128:(tt + 1) * 128], psb(4)[0:64, 512:640], [PSB[4]], ["krT"], eng=DVE)

    p1_a(0)
    for tt in range(NT):
        if tt % 4 == 1 and tt < 12:
            hstats(0, list(range(tt + 3, tt + 7)))
        p1_b(tt)
        if tt + 1 < NT:
            p1_a(tt + 1)
        p1_c(tt)

    if debug:
        dB = nc.dram_tensor("dbgB", [128, 19456], BF16, kind="ExternalOutput").ap()
        P.dma(dB[:, :], arB[:], reads=list(P.bufs.values()), writes=["w_in"], key="dbg")
        dC1 = nc.dram_tensor("dbgC1", [128, 16384], BF16, kind="ExternalOutput").ap()
        P.dma(dC1[:, :], arC[:], reads=list(P.bufs.values()), writes=["cq_bf", "ckv_bf", "u_bf0", "u_bf1", "uTs0", "uTs1"], key="dbg")
    uq_d = a_w_uq.rearrange("(k p) n -> p k n", p=128)
    uk_d = a_w_uk.rearrange("(k p) n -> p k n", p=128)
    uv_d = a_w_uv.rearrange("(k p) n -> p k n", p=128)

    def l0_head_weights(hh):
        load_w(wq, uq_d[:, :, hh * 192:(hh + 1) * 192], [128, 6, 192],
               gn[:, 8:14].unsqueeze(2).to_broadcast([128, 6, 192]), "wq")
        load_w(wk, uk_d[:, :, hh * 128:(hh + 1) * 128], [128, 4, 128],
               gn[:, 14:18].unsqueeze(2).to_broadcast([128, 4, 128]), "wk")
        load_w(wv, uv_d[:, :, hh * 128:(hh + 1) * 128], [128, 4, 128],
               gn[:, 14:18].unsqueeze(2).to_broadcast([128, 4, 128]), "wv")

    zs_d = zscr.rearrange("(t p) (g c) -> p t g c", p=128, c=128)

    pe_fence()
    op(POOL, "memset", dict(ap=V[:, :, 128:129], constant=1.0), reads=fence["cur"], writes=["V"])
    op(POOL, "memset", dict(ap=QrT[64:128, :], constant=0.0), reads=fence["cur"], writes=["QrT"])
    l0_head_weights(0)
    for hh in range(nheads):
        P.dma(zbuf, zs_d[:, :, hh, :], reads=["zscr%d" % t for t in range(NT)] + fence["cur"], writes=["zbuf"], key="zl")
        for tb in range(4):
            bk = nbank()
            for k in range(4):
                op(PE, "matmul", dict(out=psf(bk), lhsT=wk[:, k, :], rhs=ckvnT[:, k, tb * 512:(tb + 1) * 512],
                                               start=(k == 0), stop=(k == 3)),
                   reads=["wk", "ckvnT"], writes=[PSB[bk]])
            evac(KT[:, tb * 512:(tb + 1) * 512], psf(bk), [PSB[bk]], ["KT"])
        for tb in range(4):
            bk = nbank()
            for k in range(6):
                op(PE, "matmul", dict(out=psf(bk), lhsT=wq[:, k, 0:128], rhs=cqnT[:, k, tb * 512:(tb + 1) * 512],
                                               start=(k == 0), stop=(k == 5)),
                   reads=["wq", "cqnT"], writes=[PSB[bk]])
            evac(QT[:, tb * 512:(tb + 1) * 512], psf(bk), [PSB[bk]], ["QT"])
        for j in range(4):
            bk = nbank()
            for i in range(4):
                tt = 4 * j + i
                for k in range(4):
                    op(PE, "matmul", dict(out=psf(bk)[:, i * 128:(i + 1) * 128],
                                                               lhsT=ckvnT[:, k, tt * 128:(tt + 1) * 128], rhs=wv[:, k, :],
                                                               start=(k == 0), stop=(k == 3)),
                       reads=["wv", "ckvnT"], writes=[PSB[bk]])
            evac(V[:, 4 * j:4 * j + 4, 0:128], psf(bk).rearrange("p (i c) -> p i c", i=4), [PSB[bk]], ["V"])
        def qr_p(j):
            bk = (6, 7)[j % 2]
            for i in range(4):
                tt = 4 * j + i
                for k in range(6):
                    op(PE, "matmul", dict(out=psf(bk)[:, i * 64:(i + 1) * 64],
                                          lhsT=cqnT[:, k, tt * 128:(tt + 1) * 128], rhs=wq[:, k, 128:192],
                                          start=(k == 0), stop=(k == 5)),
                       reads=["wq", "cqnT"], writes=[PSB[bk]])
            xq = psf(bk)[:, 0:256].rearrange("p (i d) -> p i d", i=4)
            cs, sn = cosA[:, 4 * j:4 * j + 4, :], sinA[:, 4 * j:4 * j + 4, :]
            t1 = rtmp[:, 0:128].rearrange("p (i d) -> p i d", i=4)
            t2 = rtmp[:, 128:256].rearrange("p (i d) -> p i d", i=4)
            o1f = o1n[:].rearrange("p a b -> p (a b)")
            t3 = o1f[:, 0:128].rearrange("p (i d) -> p i d", i=4)
            t4 = o1f[:, 128:256].rearrange("p (i d) -> p i d", i=4)
            qo = kqb[:, j % 2, 0:256].rearrange("p (i d) -> p i d", i=4)
            kt_ = "kqb%d" % (j % 2)
            op(DVE, "tensor_tensor", dict(out=t1, in0=xq[:, :, 0:32], in1=cs, op=ALU.mult), reads=[PSB[bk], "cosA"], writes=["rt0"])
            op(DVE, "tensor_tensor", dict(out=t2, in0=xq[:, :, 32:64], in1=sn, op=ALU.mult), reads=[PSB[bk], "sinA"], writes=["rt1"])
            op(DVE, "tensor_tensor", dict(out=qo[:, :, 0:32], in0=t1, in1=t2, op=ALU.subtract), reads=["rt0", "rt1"], writes=[kt_])
            op(DVE, "tensor_tensor", dict(out=t3, in0=xq[:, :, 32:64], in1=cs, op=ALU.mult), reads=[PSB[bk], "cosA"], writes=["rt2"])
            op(DVE, "tensor_tensor", dict(out=t4, in0=xq[:, :, 0:32], in1=sn, op=ALU.mult), reads=[PSB[bk], "sinA"], writes=["rt3"])
            op(DVE, "tensor_tensor", dict(out=qo[:, :, 32:64], in0=t3, in1=t4, op=ALU.add), reads=["rt2", "rt3"], writes=[kt_])

        def qr_t(j):
            bt = (4, 5)[j % 2]
            kt_ = "kqb%d" % (j % 2)
            for i in range(4):
                op(PE, "transpose", dict(out=psb(bt)[0:64, i * 128:(i + 1) * 128], in_=kqb[:, j % 2, i * 64:(i + 1) * 64],
                                         identity=ident[:]),
                   reads=[kt_, "ident"], writes=[PSB[bt]])
            evac(QrT[0:64, j * 512:(j + 1) * 512], psb(bt)[0:64, 0:512], [PSB[bt]], ["QrT"])

        for j in range(4):
            qr_p(j)
            if j >= 1:
                qr_t(j - 1)
        qr_t(3)
        if hh + 1 < nheads:
            l0_head_weights(hh + 1)
        else:
            load_wo(a_w_o, gpost_a, ["cqnT", "ckvnT"])
        attention(hh, 0, 1)

    pe_fence()
    out_proj(0, a_w_o, gpost_a, last=(n_layers == 1))
    pe_fence()

    if n_layers == 2:
        for tt in range(NT):
            sl = tt % 2
            if tt % 2 == 0:
                op(DVE, "tensor_scalar", dict(out=u_bf[sl], in0=h[:, tt, :], scalar1=st[:, RSTDH + tt:RSTDH + tt + 1],
                                              scalar2=None, op0=ALU.mult),
                   reads=["h%d" % tt, "rstdh%d" % (tt // 4)], writes=["u_bf%d" % sl])
            else:
                op(ACT, "activation", dict(out=u_bf[sl], in_=h[:, tt, :], func=AF.Copy,
                                           scale=st[:, RSTDH + tt:RSTDH + tt + 1]),
                   reads=["h%d" % tt, "rstdh%d" % (tt // 4)], writes=["u_bf%d" % sl])
            bk = nbank()
            for c in range(8):
                op(PE, "transpose", dict(out=psb(bk)[:, c * 128:(c + 1) * 128], in_=u_bf[sl][:, c * 128:(c + 1) * 128],
                                         identity=ident[:]),
                   reads=["u_bf%d" % sl, "ident"], writes=[PSB[bk]])
            evac(uT[:, :, tt * 128:(tt + 1) * 128], psb(bk)[:, 0:1024].rearrange("p (c t) -> p c t", c=8), [PSB[bk]], ["uT"],
                 eng=(ACT if tt % 2 == 0 else DVE))

        kv_d = b_w_kv.rearrange("(k p) n -> p k n", p=128)
        in_d = b_w_in.rearrange("(k p) n -> p k n", p=128)
        g_kv = gn[:, 18:26].unsqueeze(2).to_broadcast([128, 8, 128])
        g_pre = gn[:, 26:34].unsqueeze(2).to_broadcast([128, 8, 128])

        def l1_head_weights(hh):
            load_w(wg[:, :, 0:128], kv_d[:, :, hh * 128:(hh + 1) * 128], [128, 8, 128], g_kv, "wg")
            load_w(wg[:, :, 128:256], in_d[:, :, hh * 128:(hh + 1) * 128], [128, 8, 128], g_pre, "wg")
            load_w(wg[:, :, 256:384], kv_d[:, :, 1024 + hh * 128:1024 + (hh + 1) * 128], [128, 8, 128], g_kv, "wg")
            load_w(wg[:, :, 384:512], in_d[:, :, 1024 + hh * 128:1024 + (hh + 1) * 128], [128, 8, 128], g_pre, "wg")

        op(POOL, "memset", dict(ap=V[:, :, 128:129], constant=1.0), reads=fence["cur"], writes=["V"])
        op(POOL, "memset", dict(ap=KT[64:128, :], constant=0.0), reads=fence["cur"], writes=["KT"])
        op(POOL, "memset", dict(ap=QrT[0:64, :], constant=0.0), reads=fence["cur"], writes=["QrT"])
        l1_head_weights(0)
        def l1_proj_p(j):
            pa = (0, 2, 6)[j % 3]
            for i in range(2):
                tt = 2 * j + i
                for (bk_, c0_) in ((pa, 0), (pa + 1, 256)):
                    for k in range(8):
                        op(PE, "matmul", dict(out=psf(bk_)[:, i * 256:(i + 1) * 256], lhsT=uT[:, k, tt * 128:(tt + 1) * 128],
                                              rhs=wg[:, k, c0_:c0_ + 256], start=(k == 0), stop=(k == 7)),
                           reads=["uT", "wg"], writes=[PSB[bk_]])
            return pa

        def l1_proj_r(j, pa):
            tt0 = 2 * j
            s0 = 2 * (j % 2)
            xa = psf(pa).rearrange("p (i c d) -> p i c d", i=2, c=4)
            kq4 = kqb[:, s0:s0 + 2, :].rearrange("p i (c d) -> p i c d", c=4)
            rest, rk = "kqb_rest%d" % (j % 2), "kqb_rope%d" % (j % 2)
            op(ACT, "copy", dict(out=kq4[:, :, :, 16:64], in_=xa[:, :, :, 16:64]), reads=[PSB[pa]], writes=[rest])
            cs = cosB[:, tt0:tt0 + 2, :].unsqueeze(2).to_broadcast([128, 2, 4, 8])
            sn = sinB[:, tt0:tt0 + 2, :].unsqueeze(2).to_broadcast([128, 2, 4, 8])
            t1, t2, t3, t4 = [rtmp[:, q_ * 64:(q_ + 1) * 64].rearrange("p (i c d) -> p i c d", i=2, c=4) for q_ in range(4)]
            op(DVE, "tensor_tensor", dict(out=t1, in0=xa[:, :, :, 0:8], in1=cs, op=ALU.mult), reads=[PSB[pa], "cosB"], writes=["rt0"])
            op(DVE, "tensor_tensor", dict(out=t2, in0=xa[:, :, :, 8:16], in1=sn, op=ALU.mult), reads=[PSB[pa], "sinB"], writes=["rt1"])
            op(DVE, "tensor_tensor", dict(out=kq4[:, :, :, 0:8], in0=t1, in1=t2, op=ALU.subtract), reads=["rt0", "rt1"], writes=[rk])
            op(DVE, "tensor_tensor", dict(out=t3, in0=xa[:, :, :, 8:16], in1=cs, op=ALU.mult), reads=[PSB[pa], "cosB"], writes=["rt2"])
            op(DVE, "tensor_tensor", dict(out=t4, in0=xa[:, :, :, 0:8], in1=sn, op=ALU.mult), reads=[PSB[pa], "sinB"], writes=["rt3"])
            op(DVE, "tensor_tensor", dict(out=kq4[:, :, :, 8:16], in0=t3, in1=t4, op=ALU.add), reads=["rt2", "rt3"], writes=[rk])
            xb = psf(pa + 1).rearrange("p (i c) -> p i c", i=2)
            op(DVE, "tensor_copy", dict(out=V[:, tt0:tt0 + 2, 0:128], in_=xb[:, :, 0:128]),
               reads=[PSB[pa + 1]] + fence["cur"], writes=["V"])
            op(ACT, "activation", dict(out=zbuf[:, tt0:tt0 + 2, :], in_=xb[:, :, 128:256], func=AF.Silu),
               reads=[PSB[pa + 1]] + fence["cur"], writes=["zbuf"])
            op(POOL, "tensor_tensor", dict(out=zbuf[:, tt0:tt0 + 2, :], in0=zbuf[:, tt0:tt0 + 2, :],
                                           in1=g2b[:].unsqueeze(1).to_broadcast([128, 2, 128]), op=ALU.mult),
               reads=["zbuf", "g2b"], writes=["zbuf"])

        def l1_proj_t(j):
            bt = 4 + j % 2
            s0 = 2 * (j % 2)
            rest, rk = "kqb_rest%d" % (j % 2), "kqb_rope%d" % (j % 2)
            for i in range(2):
                for s_ in range(2):
                    op(PE, "transpose", dict(out=psb(bt)[:, (i * 2 + s_) * 128:(i * 2 + s_ + 1) * 128],
                                             in_=kqb[:, s0 + i, s_ * 128:(s_ + 1) * 128], identity=ident[:]),
                       reads=[rest, rk, "ident"], writes=[PSB[bt]])
            pv = psb(bt)[:, 0:512].rearrange("p (i s c) -> p i s c", i=2, s=2)
            evac(KT[0:64, j * 256:(j + 1) * 256].rearrange("p (i c) -> p i c", i=2), pv[0:64, :, 0, :], [PSB[bt]], ["KT"])
            evac(QrT[64:128, j * 256:(j + 1) * 256].rearrange("p (i c) -> p i c", i=2), pv[64:128, :, 0, :], [PSB[bt]], ["QrT"])
            evac(QT[:, j * 256:(j + 1) * 256].rearrange("p (i c) -> p i c", i=2), pv[:, :, 1, :], [PSB[bt]], ["QT"])

        for hh in range(8):
            for j in range(8):
                pa_ = l1_proj_p(j)
                l1_proj_r(j, pa_)
                if j >= 1:
                    l1_proj_t(j - 1)
            l1_proj_t(7)
            if hh + 1 < 8:
                l1_head_weights(hh + 1)
            else:
                load_wo(b_w_o, gpost_b, ["uT"])
            attention(hh, 1, 2)
        pe_fence()
        out_proj(1, b_w_o, gpost_b, last=True)

    fk = ["out"]
    if debug:
        dA = nc.dram_tensor("dbgA", [128, 22528], BF16, kind="ExternalOutput").ap()
        dC = nc.dram_tensor("dbgC", [128, 16384], BF16, kind="ExternalOutput").ap()
        dS = nc.dram_tensor("dbgS", [128, 64], F32, kind="ExternalOutput").ap()
        dT = nc.dram_tensor("dbgT", [128, 4 * NT * 32], F32, kind="ExternalOutput").ap()
        allb = list(P.bufs.values())
        P.dma(dA[:, :], arA[:], reads=allb, key="dbg")
        P.dma(dC[:, :], arC[:], reads=allb, key="dbg")
        P.dma(dS[:, :], st[:], reads=allb, key="dbg")
        P.dma(dT[:, 0:512], cosA[:].rearrange("p t f -> p (t f)"), reads=allb, key="dbg")
        P.dma(dT[:, 512:1024], sinA[:].rearrange("p t f -> p (t f)"), reads=allb, key="dbg")
        P.dma(dT[:, 1024:1152], cosB[:].rearrange("p t f -> p (t f)"), reads=allb, key="dbg")
        P.dma(dT[:, 1152:1280], sinB[:].rearrange("p t f -> p (t f)"), reads=allb, key="dbg")
        fk.append("dbg")
    P.emit(final_wait_keys=fk)
    nc._op_counts = {e: sum(1 for o in P.ops if o.eng == e) for e in ENGINES}
    return nc


_CACHE = {}


def _prep_inputs(inputs):
    f = lambda a: np.ascontiguousarray(np.asarray(a), dtype=np.float32)
    x = f(inputs["x"])
    pos = np.asarray(inputs["positions"]).astype(np.int32)

    def chunked(g):
        g = f(g).reshape(-1, 128)
        return np.ascontiguousarray(g.T)

    gains = np.concatenate([chunked(inputs["a_pre_g"][0]), chunked(inputs["a_q_norm_g"][0]),
                            chunked(inputs["a_kv_norm_g"][0]), chunked(inputs["b_kv_norm_g"]),
                            chunked(inputs["b_pre_g"][0])], axis=1)
    rep = lambda v, n=128: np.ascontiguousarray(np.broadcast_to(f(v).reshape(1, -1), (n, f(v).size)))
    invA = (10000.0 ** (-np.arange(0, 64, 2, dtype=np.float32) / np.float32(64))).astype(np.float32)
    invB = (500000.0 ** (-np.arange(0, 16, 2, dtype=np.float32) / np.float32(16))).astype(np.float32)
    cst = rep(np.concatenate([invA, invB]))
    shared = {
        "a_w_in": f(inputs["a_w_in"][0]), "a_w_uq": f(inputs["a_w_uq"][0]), "a_w_uk": f(inputs["a_w_uk"][0]),
        "a_w_uv": f(inputs["a_w_uv"][0]), "a_w_o": f(inputs["a_w_o"][0]), "b_w_kv": f(inputs["b_w_kv"]),
        "b_w_in": f(inputs["b_w_in"][0]), "b_w_o": f(inputs["b_w_o"][0]),
        "gains": gains, "gpost_a": rep(inputs["a_post_g"][0]), "gpost_b": rep(inputs["b_post_g"][0]),
        "lam_in": rep(inputs["b_lambda"][0]), "subln_in": rep(inputs["b_subln_g"][0]),
        "cst": cst, "ident_in": np.eye(128, dtype=np.float32),
    }
    maps = []
    for b in range(8):
        m = dict(shared)
        m["x"] = x[b]
        m["pos"] = np.ascontiguousarray(pos[b].reshape(NT, 128).T)
        maps.append(m)
    return maps


def kernel(**inputs):
    if "nc" not in _CACHE:
        _CACHE["nc"] = build(2)
    maps = _prep_inputs(inputs)
    res = run_bass_kernel_spmd(_CACHE["nc"], maps, core_ids=list(range(8)))
    return np.stack([np.asarray(res.results[i]["out"], dtype=np.float32) for i in range(8)], axis=0)
```

```python
import math
import numpy as np
import concourse.bass as bass
import concourse.mybir as mybir
from concourse.bass_utils import run_bass_kernel_spmd
from contextlib import ExitStack

F32 = mybir.dt.float32
BF16 = mybir.dt.bfloat16
I32 = mybir.dt.int32
AF = mybir.ActivationFunctionType
ALU = mybir.AluOpType

PE, ACT, DVE, POOL, SP = "tensor", "scalar", "vector", "gpsimd", "sync"
ENGINES = (PE, ACT, DVE, POOL, SP)

S = 2048
D = 1024
NT = 16
EPS = 1e-6
LAM_INIT = 0.8 - 0.6 * math.exp(-0.3 * 1)


class Buf:
    __slots__ = ("name", "last_w", "readers", "excl")

    def __init__(self, name):
        self.name = name
        self.last_w = None
        self.readers = []
        self.excl = False


class Op:
    __slots__ = ("eng", "fn", "deps", "is_dma", "sem_key", "count", "signal")

    def __init__(self, eng, fn, is_dma=False, sem_key=None):
        self.eng = eng
        self.fn = fn
        self.deps = []
        self.is_dma = is_dma
        self.sem_key = sem_key
        self.count = None
        self.signal = False


class Prog:
    def __init__(self, nc):
        self.nc = nc
        self.ops = []
        self.es = ExitStack()
        self.bufs = {}
        self.dummy = None

    def B(self, name):
        b = self.bufs.get(name)
        if b is None:
            b = self.bufs[name] = Buf(name)
        return b

    def sb(self, name, shape, dt):
        return self.es.enter_context(self.nc.sbuf_tensor(name, list(shape), dt))

    def ps(self, name, shape, dt):
        return self.es.enter_context(self.nc.psum_tensor(name, list(shape), dt))

    def op(self, eng, meth, kw=None, reads=(), writes=(), dma_key=None):
        if isinstance(meth, str):
            fn = (lambda e, meth=meth, kw=kw: getattr(e, meth)(**kw))
        else:
            fn = meth
        o = Op(eng, fn, is_dma=dma_key is not None, sem_key=dma_key)
        seen = set()

        def add(d, kind):
            if d is None or d is o:
                return
            k = (id(d), kind)
            if k in seen:
                return
            seen.add(k)
            o.deps.append((d, kind))

        reads = [self.B(b) if isinstance(b, str) else b for b in reads]
        writes = [self.B(b) if isinstance(b, str) else b for b in writes]
        for b in reads:
            add(b.last_w, "RAW")
            if b.excl:
                for r in b.readers:
                    if r.eng != eng:
                        add(r, "WAR")
        for b in writes:
            add(b.last_w, "WAW")
            for r in b.readers:
                add(r, "WAR")
        for b in reads:
            b.readers.append(o)
        for b in writes:
            b.last_w = o
            b.readers = []
        self.ops.append(o)
        if isinstance(kw, dict) and kw.get("accum_out") is not None and self.dummy is not None:
            if eng == ACT:
                return self.op(ACT, "copy", dict(out=self.dummy[1], in_=self.dummy[0]), reads=["eps"], writes=writes)
            if eng == DVE:
                return self.op(DVE, "tensor_copy", dict(out=self.dummy[3], in_=self.dummy[2]), reads=["eps"], writes=writes)
        return o

    def dma(self, out, in_, reads=(), writes=(), key=None, **kw):
        return self.op(SP, "dma_start", dict(out=out, in_=in_, **kw), reads, writes, dma_key=key)

    def emit(self, final_wait_keys=()):
        nc = self.nc
        for o in self.ops:
            for d, kind in o.deps:
                if d.is_dma:
                    continue
                if d.eng == o.eng and (d.eng == PE or kind != "RAW"):
                    continue
                d.signal = True
        eng_cnt = {e: 0 for e in ENGINES}
        key_cnt = {}
        for o in self.ops:
            if o.is_dma:
                key_cnt[o.sem_key] = key_cnt.get(o.sem_key, 0) + 16
                o.count = key_cnt[o.sem_key]
            elif o.signal:
                eng_cnt[o.eng] += 1
                o.count = eng_cnt[o.eng]
        sems = {}
        for e in (PE, ACT, DVE, POOL):
            sems[e] = self.es.enter_context(nc.semaphore("s_" + e))
        for k in key_cnt:
            sems[("dma", k)] = self.es.enter_context(nc.semaphore("d_%s" % (k,)))
        per_eng = {e: [o for o in self.ops if o.eng == e] for e in ENGINES}
        final = [(sems[("dma", k)], key_cnt[k]) for k in final_wait_keys]

        def run(e, handle):
            waited = {}
            for o in per_eng[e]:
                for d, kind in o.deps:
                    if d.is_dma:
                        kk = ("dma", d.sem_key)
                    else:
                        if d.eng == e and (e == PE or kind != "RAW"):
                            continue
                        kk = d.eng
                    if waited.get(kk, 0) >= d.count:
                        continue
                    waited[kk] = d.count
                    handle.wait_ge(sems[kk], d.count)
                ins = o.fn(handle)
                if o.is_dma:
                    ins.then_inc(sems[("dma", o.sem_key)], 16)
                elif o.signal:
                    ins.then_inc(sems[e], 1)
            if e == SP:
                for s, c in final:
                    handle.wait_ge(s, c)

        with nc.Block() as block:
            @block.tensor
            def _(h):
                run(PE, h)

            @block.scalar
            def _(h):
                run(ACT, h)

            @block.vector
            def _(h):
                run(DVE, h)

            @block.gpsimd
            def _(h):
                run(POOL, h)

            @block.sync
            def _(h):
                run(SP, h)
        self.es.close()


def build(n_layers=2, debug=False, nheads=8):
    nc = bass.Bass("TRN2", target_bir_lowering=False)

    def din(name, shape, dt=F32):
        return nc.dram_tensor(name, list(shape), dt, kind="ExternalInput").ap()

    x = din("x", [S, D])
    pos = din("pos", [128, NT], I32)
    a_w_in = din("a_w_in", [D, 2368])
    a_w_uq = din("a_w_uq", [768, 1536])
    a_w_uk = din("a_w_uk", [512, 1024])
    a_w_uv = din("a_w_uv", [512, 1024])
    a_w_o = din("a_w_o", [D, D])
    b_w_kv = din("b_w_kv", [D, 2048])
    b_w_in = din("b_w_in", [D, 2048])
    b_w_o = din("b_w_o", [D, D])
    gains = din("gains", [128, 34])
    gpost_a = din("gpost_a", [128, D])
    gpost_b = din("gpost_b", [128, D])
    lam_in = din("lam_in", [128, 256])
    subln_in = din("subln_in", [128, 128])
    cst = din("cst", [128, 40])
    ident_in = din("ident_in", [128, 128])
    out = nc.dram_tensor("out", [S, D], F32, kind="ExternalOutput").ap()
    zscr = nc.dram_tensor("zscr", [S, D], F32, kind="Internal").ap()

    P = Prog(nc)
    op = P.op

    h = P.sb("h", [128, NT, D], F32)
    arA = P.sb("arA", [128, 22528], BF16)
    arB = P.sb("arB", [128, 19456], BF16)
    arC = P.sb("arC", [128, 16384], BF16)
    stg = [P.sb("stg%d" % i, [128, 1184], F32) for i in range(2)]
    ident = P.sb("ident", [128, 128], BF16)
    cosA = P.sb("cosA", [128, NT, 32], F32)
    sinA = P.sb("sinA", [128, NT, 32], F32)
    cosB = P.sb("cosB", [128, NT, 8], F32)
    sinB = P.sb("sinB", [128, NT, 8], F32)
    gpost = P.sb("gpost", [128, D], F32)
    ET = [P.sb("ET%d" % i, [128, 512], BF16) for i in range(4)]
    o1n = P.sb("o1n", [128, 4, 128], F32)
    rtmp = P.sb("rtmp", [128, 256], F32)
    kqb = P.sb("kqb", [128, 4, 256], BF16)
    gn = P.sb("gn", [128, 34], F32)
    junkd = P.sb("junkd", [128, 128], BF16)
    mkq = P.sb("mkq", [1, 256], BF16)
    g2b = P.sb("g2b", [128, 128], F32)
    st = P.sb("st", [128, 64], F32)
    RSTDH = 0
    SSH = 16
    SSQ = 32
    LNT = 34
    RSQ = 38
    RIN = 40
    SSD = 44
    RSD = 48
    NLAM = 52
    EPSC = 53
    LT = 54
    S2 = 58

    def view(arena, boff, shape, dt):
        n = 1
        for s_ in shape[1:]:
            n *= s_
        ne = n * (2 if dt == F32 else 1)
        a = arena[:, boff // 2: boff // 2 + ne]
        if dt == F32:
            a = a.bitcast(F32)
        if len(shape) == 3:
            a = a.rearrange("p (a b) -> p a b", a=shape[1])
        return a

    cqnT = view(arA, 0, [128, 6, S], BF16)
    ckvnT = view(arA, 24576, [128, 4, S], BF16)
    krT = view(arA, 40960, [128, S], BF16)
    uT = view(arA, 0, [128, 8, S], BF16)
    w_in = view(arB, 0, [128, 8, 2368], BF16)
    KT = view(arB, 0, [128, S], BF16)
    QT = view(arB, 4096, [128, S], BF16)
    QrT = view(arB, 8192, [128, S], BF16)
    V = view(arB, 12288, [128, NT, 129], BF16)
    zbuf = view(arB, 16448, [128, NT, 128], F32)
    wq = view(arB, 24640, [128, 6, 192], BF16)
    wk = view(arB, 24640 + 2304, [128, 4, 128], BF16)
    wv = view(arB, 24640 + 3328, [128, 4, 128], BF16)
    wg = view(arB, 24640, [128, 8, 512], BF16)
    w_o = view(arA, 0, [128, 8, D], BF16)
    tmpr = view(arB, 16384, [128, D], F32)
    junk4 = view(arB, 20480, [128, D], BF16)
    aT = [view(arB, 22528 + i * 2048, [128, 8, 128], BF16) for i in range(2)]
    abuf = view(arC, 0, [128, NT, D], BF16)
    u_bf = [view(arC, i * 2048, [128, D], BF16) for i in range(2)]
    uTs = [view(arC, 4096 + i * 2048, [128, 8, 128], BF16) for i in range(2)]
    cq_bf = view(arC, 8192, [128, 768], BF16)
    ckv_bf = view(arC, 9728, [128, 512], BF16)
    kr_bf = view(arC, 10752, [128, 64], BF16)
    junk1 = view(arC, 11264, [128, D], BF16)
    zstage = [view(arC, 13312 + i * 4096, [128, D], F32) for i in range(2)]
    t_posf = view(arC, 0, [128, NT], F32)
    t_ang = view(arC, 1024, [128, NT * 40], F32)
    t_y = view(arC, 4096, [128, NT * 40], F32)
    t_k = view(arC, 7168, [128, NT * 40], F32)
    t_r = view(arC, 10240, [128, NT * 40], F32)
    t_m = view(arC, 13312, [128, NT * 40], F32)
    t_sc = view(arC, 16384, [128, NT, 40], F32)
    t_lam = view(arC, 20480, [128, 256], F32)
    t_sub = view(arC, 22528, [128, 128], F32)
    t_id = view(arC, 24576, [128, 128], F32)
    t_pos = arC[:, 13000:13000 + 2 * NT].bitcast(I32)
    t_ki = view(arC, 28672, [128, NT * 40], F32).bitcast(I32)

    PS = P.ps("PS", [128, 8, 512], F32)
    PSf = PS[:].rearrange("p b n -> p (b n)")
    P.dummy = (st[:, EPSC:EPSC + 1], st[:, 63:64], st[:, EPSC:EPSC + 1], st[:, 61:62])
    PSB = [P.B("psb%d" % i) for i in range(8)]
    for b_ in PSB:
        b_.excl = True

    def psf(i):
        return PS[:, i, :]

    def psb(i):
        return PS[:, i, :].bitcast(BF16)

    P.dma(gn[:], gains[:, :], writes=["gn"], key="c0")
    P.dma(t_pos, pos[:, :], writes=["t_pos"], key="c1")
    P.dma(t_sc[:, 0, :], cst[:, :], writes=["t_sc0"], key="c2")
    P.dma(t_lam, lam_in[:, :], writes=["t_lam"], key="c3")
    P.dma(t_sub, subln_in[:, :], writes=["t_sub"], key="c4")
    P.dma(t_id, ident_in[:, :], writes=["t_id"], key="c5")

    op(DVE, "tensor_copy", dict(out=ident[:], in_=t_id), reads=["t_id"], writes=["ident"])
    op(DVE, "memset", dict(ap=st[:, EPSC:EPSC + 1], constant=EPS), writes=["eps"])
    op(DVE, "memset", dict(ap=mkq[0:1, 0:64], constant=0.0), writes=["mkq"])
    op(DVE, "memset", dict(ap=mkq[0:1, 64:128], constant=1.0), writes=["mkq"])
    op(DVE, "memset", dict(ap=mkq[0:1, 128:192], constant=-30000.0), writes=["mkq"])
    op(DVE, "memset", dict(ap=mkq[0:1, 192:256], constant=0.0), writes=["mkq"])
    op(DVE, "tensor_copy", dict(out=t_posf, in_=t_pos), reads=["t_pos"], writes=["t_posf"])
    op(DVE, "tensor_copy", dict(out=rtmp[:, 0:40], in_=t_sc[:, 0, :]), reads=["t_sc0"], writes=["inv"])
    ang3 = t_ang.rearrange("p (t f) -> p t f", t=NT)
    for tt in range(NT):
        op(DVE, "tensor_scalar", dict(out=ang3[:, tt, :], in0=rtmp[:, 0:40], scalar1=t_posf[:, tt:tt + 1],
                                                 scalar2=None, op0=ALU.mult),
           reads=["inv", "t_posf"], writes=["ang"])
    TWO_PI = 2.0 * math.pi
    C1 = 6.28125
    C2 = TWO_PI - C1
    PI_IN = 3.1415925

    def trig(shift, dst_tok):
        op(DVE, "tensor_scalar", dict(out=t_y, in0=t_ang, scalar1=float(shift), scalar2=None, op0=ALU.add),
           reads=["ang"], writes=["t_y"])
        op(DVE, "tensor_scalar", dict(out=t_ki, in0=t_y, scalar1=float(1.0 / TWO_PI), scalar2=None, op0=ALU.mult),
           reads=["t_y"], writes=["t_ki"])
        op(DVE, "tensor_copy", dict(out=t_k, in_=t_ki), reads=["t_ki"], writes=["t_k"])
        op(DVE, "scalar_tensor_tensor", dict(out=t_r, in0=t_k, scalar=float(-C1), in1=t_y, op0=ALU.mult, op1=ALU.add),
           reads=["t_k", "t_y"], writes=["t_r"])
        op(DVE, "scalar_tensor_tensor", dict(out=t_y, in0=t_k, scalar=float(-C2), in1=t_r, op0=ALU.mult, op1=ALU.add),
           reads=["t_k", "t_r"], writes=["t_y"])
        op(DVE, "tensor_scalar", dict(out=t_m, in0=t_y, scalar1=0.0, scalar2=None, op0=ALU.is_lt),
           reads=["t_y"], writes=["t_m"])
        op(DVE, "scalar_tensor_tensor", dict(out=t_r, in0=t_m, scalar=float(TWO_PI), in1=t_y, op0=ALU.mult, op1=ALU.add),
           reads=["t_m", "t_y"], writes=["t_r"])
        op(DVE, "tensor_scalar", dict(out=t_m, in0=t_r, scalar1=float(TWO_PI), scalar2=None, op0=ALU.is_ge),
           reads=["t_r"], writes=["t_m"])
        op(DVE, "scalar_tensor_tensor", dict(out=t_y, in0=t_m, scalar=float(-TWO_PI), in1=t_r, op0=ALU.mult, op1=ALU.add),
           reads=["t_m", "t_r"], writes=["t_y"])
        op(DVE, "tensor_scalar", dict(out=t_r, in0=t_y, scalar1=float(-math.pi), scalar2=float(-PI_IN),
                                          op0=ALU.add, op1=ALU.max),
           reads=["t_y"], writes=["t_r"])
        op(DVE, "tensor_scalar", dict(out=t_y, in0=t_r, scalar1=float(PI_IN), scalar2=None, op0=ALU.min),
           reads=["t_r"], writes=["t_y"])
        op(ACT, "activation", dict(out=t_sc[:].rearrange("p t f -> p (t f)"), in_=t_y, func=AF.Sin),
           reads=["t_y"], writes=["t_sc"])

    trig(math.pi, None)
    op(DVE, "tensor_copy", dict(out=sinA[:], in_=t_sc[:, :, 0:32]), reads=["t_sc"], writes=["sinA"])
    op(DVE, "tensor_copy", dict(out=sinB[:], in_=t_sc[:, :, 32:40]), reads=["t_sc"], writes=["sinB"])
    trig(1.5 * math.pi, None)
    op(DVE, "tensor_copy", dict(out=cosA[:], in_=t_sc[:, :, 0:32]), reads=["t_sc"], writes=["cosA"])
    op(DVE, "tensor_copy", dict(out=cosB[:], in_=t_sc[:, :, 32:40]), reads=["t_sc"], writes=["cosB"])

    for q_ in range(2):
        op(DVE, "tensor_tensor", dict(out=rtmp[:, 64:128], in0=t_lam[:, q_ * 128:q_ * 128 + 64],
                                      in1=t_lam[:, q_ * 128 + 64:q_ * 128 + 128], op=ALU.mult),
           reads=["t_lam"], writes=["rt64"])
        op(DVE, "tensor_scalar", dict(out=rtmp[:, 128:192], in0=rtmp[:, 64:128], scalar1=1.0, scalar2=None, op0=ALU.mult,
                                      op1=ALU.add, accum_out=st[:, LT + q_:LT + q_ + 1]),
           reads=["rt64"], writes=["lt%d" % q_, "rt128"])
    op(ACT, "activation", dict(out=st[:, LT + 2:LT + 4], in_=st[:, LT:LT + 2], func=AF.Exp),
       reads=["lt0", "lt1"], writes=["lt23"])
    op(DVE, "tensor_tensor", dict(out=st[:, LT:LT + 1], in0=st[:, LT + 3:LT + 4], in1=st[:, LT + 2:LT + 3], op=ALU.subtract),
       reads=["lt23"], writes=["lt0"])
    op(DVE, "tensor_scalar", dict(out=st[:, NLAM:NLAM + 1], in0=st[:, LT:LT + 1], scalar1=float(-LAM_INIT), scalar2=None,
                                      op0=ALU.add),
       reads=["lt0"], writes=["nlam"])
    op(DVE, "tensor_scalar", dict(out=g2b[:], in0=t_sub, scalar1=float(1.0 - LAM_INIT), scalar2=None, op0=ALU.mult),
       reads=["t_sub"], writes=["g2b"])

    rr = {"i": 0, "ev": 0, "stg": 0}
    fence = {"cur": [], "n": 0}

    def nbank(lo=0, hi=8):
        b = lo + rr["i"] % (hi - lo)
        rr["i"] += 1
        return b

    def evac(out_ap, in_ap, reads, writes, eng=None):
        if eng is None:
            eng = ACT if rr["ev"] % 2 == 0 else DVE
            rr["ev"] += 1
        reads = list(reads) + fence["cur"]
        if eng == ACT:
            op(ACT, "copy", dict(out=out_ap, in_=in_ap), reads=reads, writes=writes)
        else:
            op(DVE, "tensor_copy", dict(out=out_ap, in_=in_ap), reads=reads, writes=writes)

    def rstd_calc(ss_ap, n, out_ap, tmp_ap, reads, writes, tmpname):
        op(ACT, "activation", dict(out=tmp_ap, in_=ss_ap, func=AF.Ln, scale=float(1.0 / n), bias=st[:, EPSC:EPSC + 1]),
           reads=list(reads) + ["eps"], writes=[tmpname])
        op(ACT, "activation", dict(out=out_ap, in_=tmp_ap, func=AF.Exp, scale=-0.5),
           reads=[tmpname], writes=writes)

    def pe_fence():
        fence["n"] += 1
        n_ = fence["n"]
        op(PE, "transpose", dict(out=psb(7)[0:32, 0:32], in_=ident[0:32, 0:32], identity=ident[0:32, 0:32]),
           reads=["ident"], writes=[PSB[7], "fP%d" % n_])
        op(DVE, "tensor_copy", dict(out=st[:, 61:62], in_=st[:, EPSC:EPSC + 1]), reads=["eps"], writes=["fD%d" % n_])
        op(ACT, "copy", dict(out=st[:, 63:64], in_=st[:, EPSC:EPSC + 1]), reads=["eps"], writes=["fA%d" % n_])
        op(POOL, "memset", dict(ap=st[:, 59:60], constant=0.0), writes=["fG%d" % n_])
        fence["cur"] = ["fP%d" % n_, "fD%d" % n_, "fA%d" % n_, "fG%d" % n_]

    def load_w(dst, src, shape, gain_ap, dst_tok):
        s = rr["stg"] % 2
        rr["stg"] += 1
        n = 1
        for v_ in shape[1:]:
            n *= v_
        sv = stg[s][:, 0:n]
        if len(shape) == 3:
            sv = sv.rearrange("p (a b) -> p a b", a=shape[1])
        P.dma(sv, src, writes=["stg%d" % s], key="stg%d" % s)
        if gain_ap is None:
            op(POOL, "tensor_copy", dict(out=dst, in_=sv), reads=["stg%d" % s] + fence["cur"],
               writes=dst_tok if isinstance(dst_tok, list) else [dst_tok])
        else:
            op(POOL, "tensor_tensor", dict(out=dst, in0=sv, in1=gain_ap, op=ALU.mult),
               reads=["stg%d" % s, "gn"] + fence["cur"], writes=dst_tok if isinstance(dst_tok, list) else [dst_tok])

    def hsq(layer, tt):
        jk = junk1 if layer == 0 else junk4
        op(ACT, "activation", dict(out=jk, in_=h[:, tt, :], func=AF.Square, accum_out=st[:, SSH + tt:SSH + tt + 1]),
           reads=["h%d" % tt] + fence["cur"], writes=["junk1" if layer == 0 else "junk4", "ssh%d" % (tt // 4)])

    def hrstd(g):
        t0_, t1_ = 4 * g, 4 * g + 4
        rstd_calc(st[:, SSH + t0_:SSH + t1_], D, st[:, RSTDH + t0_:RSTDH + t1_], st[:, RSTDH + t0_:RSTDH + t1_],
                  ["ssh%d" % g], ["rstdh%d" % g], "rstdh_tmp%d" % g)

    def hstats(layer, tiles):
        for tt in tiles:
            hsq(layer, tt)
        hrstd(tiles[0] // 4)

    def attention(hh, layer, nmaps):
        scale = (192 ** -0.5) if layer == 0 else 0.125
        cnt = {"s": 0, "e": 0}
        pending = []
        for qb in range(4):
            nk = 4 * qb + 4
            for m in range(nmaps):
                def s_mm(kt, sbank):
                    c0 = 128 * max(0, kt - 4 * qb)
                    o_ap = psf(sbank)[:, c0:512]
                    diag = kt >= 4 * qb
                    if layer == 0:
                        op(PE, "matmul", dict(out=o_ap, lhsT=KT[:, kt * 128:(kt + 1) * 128],
                                              rhs=QT[:, qb * 512 + c0:(qb + 1) * 512], start=True, stop=False),
                           reads=["KT", "QT"], writes=[PSB[sbank]])
                        op(PE, "matmul", dict(out=o_ap, lhsT=krT[:, kt * 128:(kt + 1) * 128],
                                              rhs=QrT[:, qb * 512 + c0:(qb + 1) * 512], start=False, stop=not diag),
                           reads=["krT", "QrT"], writes=[PSB[sbank]])
                    else:
                        kp = KT if m == 0 else QrT
                        op(PE, "matmul", dict(out=o_ap, lhsT=kp[:, kt * 128:(kt + 1) * 128],
                                              rhs=QT[:, qb * 512 + c0:(qb + 1) * 512], start=True, stop=not diag),
                           reads=["KT", "QrT", "QT"], writes=[PSB[sbank]])
                    if diag:
                        op(PE, "matmul", dict(out=psf(sbank)[:, c0:c0 + 128], lhsT=mkq[0:1, 0:128], rhs=mkq[0:1, 128:256],
                                              start=False, stop=True),
                           reads=["mkq"], writes=[PSB[sbank]])

                def exp_op(kt, sbank, slot):
                    c0 = 128 * max(0, kt - 4 * qb)
                    et = ET[slot]
                    op(ACT, "activation", dict(out=et[:, c0:512], in_=psf(sbank)[:, c0:512], func=AF.Exp,
                                               scale=float(scale)),
                       reads=[PSB[sbank]], writes=["ET%d" % slot])

                def pv_op(kt, slot):
                    et = ET[slot]
                    for qt in range(max(0, kt - 4 * qb), 4):
                        last = (kt == 4 * qb + qt)
                        op(PE, "matmul", dict(out=psf(2 + qt)[:, 0:129], lhsT=et[:, qt * 128:(qt + 1) * 128],
                                              rhs=V[:, kt, :], start=(kt == 0), stop=last),
                           reads=["ET%d" % slot, "V"], writes=[PSB[2 + qt]])
                        if last:
                            epilogue(qt)

                def epilogue(qt):
                    tt = 4 * qb + qt
                    ob = psf(2 + qt)
                    rin = st[:, RIN + qt:RIN + qt + 1]
                    op(DVE, "reciprocal", dict(out=rin, in_=ob[:, 128:129]), reads=[PSB[2 + qt]], writes=["rin%d" % qt])
                    if layer == 0:
                        op(DVE, "scalar_tensor_tensor", dict(out=abuf[:, tt, hh * 128:(hh + 1) * 128], in0=ob[:, 0:128],
                                                                 scalar=rin, in1=zbuf[:, tt, :], op0=ALU.mult, op1=ALU.mult),
                           reads=[PSB[2 + qt], "rin%d" % qt, "zbuf"], writes=["abuf%d" % tt])
                    elif m == 0:
                        op(DVE, "tensor_scalar", dict(out=o1n[:, qt, :], in0=ob[:, 0:128], scalar1=rin, scalar2=None,
                                                          op0=ALU.mult),
                           reads=[PSB[2 + qt], "rin%d" % qt], writes=["o1n%d" % qt])
                    else:
                        s2 = st[:, S2 + qt:S2 + qt + 1]
                        op(DVE, "tensor_tensor", dict(out=s2, in0=rin, in1=st[:, NLAM:NLAM + 1], op=ALU.mult),
                           reads=["rin%d" % qt, "nlam"], writes=["s2%d" % qt])
                        op(DVE, "scalar_tensor_tensor", dict(out=o1n[:, qt, :], in0=ob[:, 0:128], scalar=s2,
                                                                 in1=o1n[:, qt, :], op0=ALU.mult, op1=ALU.add),
                           reads=[PSB[2 + qt], "s2%d" % qt, "o1n%d" % qt], writes=["o1n%d" % qt])
                        op(DVE, "scalar_tensor_tensor", dict(out=rtmp[:, 0:128], in0=o1n[:, qt, :], scalar=1.0,
                                                             in1=o1n[:, qt, :], op0=ALU.mult, op1=ALU.mult,
                                                             accum_out=st[:, SSD + qt:SSD + qt + 1]),
                           reads=["o1n%d" % qt], writes=["ssd", "rt0"])

                SBK = (0, 1, 6)
                base = cnt["s"]
                for kt in range(min(2, nk)):
                    s_mm(kt, SBK[(base + kt) % 3])
                for kt in range(nk):
                    slot = cnt["e"] % 4
                    cnt["e"] += 1
                    exp_op(kt, SBK[(base + kt) % 3], slot)
                    if kt + 2 < nk:
                        s_mm(kt + 2, SBK[(base + kt + 2) % 3])
                    pv_op(kt, slot)
                    if kt == 1 and pending:
                        pending.pop(0)()
                cnt["s"] += nk
                if layer == 1 and m == 1:
                    def finalize(qb=qb):
                        rstd_calc(st[:, SSD:SSD + 4], 128, st[:, RSD:RSD + 4], st[:, LNT:LNT + 4], ["ssd"], ["rsd"], "lnt")
                        for qt in range(4):
                            tt = 4 * qb + qt
                            op(DVE, "scalar_tensor_tensor", dict(
                                out=abuf[:, tt, hh * 128:(hh + 1) * 128], in0=o1n[:, qt, :], scalar=st[:, RSD + qt:RSD + qt + 1],
                                in1=zbuf[:, tt, :], op0=ALU.mult, op1=ALU.mult),
                               reads=["o1n%d" % qt, "rsd", "zbuf"], writes=["abuf%d" % tt])
                    pending.append(finalize)
        while pending:
            pending.pop(0)()

    def load_wo(w_dram, gp_dram, dead_toks):
        P.dma(gpost[:], gp_dram[:, :], writes=["gpost"], key="gp")
        for k in range(8):
            load_w(w_o[:, k, :], w_dram[k * 128:(k + 1) * 128, :], [128, D], None, ["w_o"] + dead_toks)

    def out_proj(layer, w_dram, gp_dram, last):
        def stage_t(tt):
            sl = tt % 2
            tb = tt % 2
            for c in range(8):
                op(PE, "transpose", dict(out=psb(tb)[:, c * 128:(c + 1) * 128], in_=abuf[:, tt, c * 128:(c + 1) * 128],
                                         identity=ident[:]),
                   reads=["abuf%d" % tt, "ident"], writes=[PSB[tb]])
            evac(aT[sl][:].rearrange("p c t -> p (c t)"), psb(tb)[:, 0:1024], [PSB[tb]], ["aT%d" % sl])

        def stage_m(tt):
            sl = tt % 2
            pb = 2 + 2 * (tt % 3)
            for half in range(2):
                for k in range(8):
                    op(PE, "matmul", dict(out=psf(pb + half), lhsT=aT[sl][:, k, :],
                                          rhs=w_o[:, k, half * 512:(half + 1) * 512],
                                          start=(k == 0), stop=(k == 7)),
                       reads=["aT%d" % sl, "w_o"], writes=[PSB[pb + half]])
            pso = PSf[:, pb * 512:pb * 512 + 1024]
            q_ = tt % 2
            op(ACT, "activation", dict(out=junk4, in_=pso, func=AF.Square, accum_out=st[:, SSQ + q_:SSQ + q_ + 1]),
               reads=[PSB[pb], PSB[pb + 1]] + fence["cur"], writes=["junk4", "sso%d" % q_])
            rstd_calc(st[:, SSQ + q_:SSQ + q_ + 1], D, st[:, RSQ + q_:RSQ + q_ + 1], st[:, LNT + q_:LNT + q_ + 1],
                      ["sso%d" % q_], ["rso%d" % q_], "lnt%d" % q_)
            op(DVE, "scalar_tensor_tensor", dict(out=tmpr, in0=pso, scalar=st[:, RSQ + q_:RSQ + q_ + 1], in1=gpost[:],
                                                 op0=ALU.mult, op1=ALU.mult),
               reads=[PSB[pb], PSB[pb + 1], "rso%d" % q_, "gpost"] + fence["cur"], writes=["tmpr"])
            op(POOL, "tensor_tensor", dict(out=h[:, tt, :], in0=h[:, tt, :], in1=tmpr, op=ALU.add),
               reads=["tmpr", "h%d" % tt], writes=["h%d" % tt])
            if last:
                P.dma(out[tt * 128:(tt + 1) * 128, :], h[:, tt, :], reads=["h%d" % tt], key="out")

        def next_stats(t_):
            hsq(1, t_)
            if t_ % 4 == 3:
                hrstd(t_ // 4)

        stage_t(0)
        for tt in range(NT):
            if tt + 1 < NT:
                stage_t(tt + 1)
            stage_m(tt)
            if not last and n_layers == 2 and tt >= 2:
                next_stats(tt - 2)
        if not last and n_layers == 2:
            next_stats(NT - 2)
            next_stats(NT - 1)

    def xload(tiles):
        for tt in tiles:
            P.dma(h[:, tt, :], x[tt * 128:(tt + 1) * 128, :], writes=["h%d" % tt], key="x%d" % tt)

    def w_in_load(ks):
        n_ = 0
        for k in ks:
            for hf in range(2):
                dst = w_in[:, k, hf * 1184:(hf + 1) * 1184]
                srcw = a_w_in[k * 128:(k + 1) * 128, hf * 1184:(hf + 1) * 1184]
                gain = gn[:, k:k + 1].to_broadcast([128, 1184])
                if n_ < 8:
                    sv = view(arA, n_ * 4736, [128, 1184], F32)
                    P.dma(sv, srcw, writes=["stgA%d" % n_], key="stgA%d" % n_)
                    op(POOL, "tensor_tensor", dict(out=dst, in0=sv, in1=gain, op=ALU.mult),
                       reads=["stgA%d" % n_, "gn"], writes=["w_in"])
                else:
                    load_w(dst, srcw, [128, 1184], gain, "w_in")
                n_ += 1

    xload(range(0, 4))
    w_in_load(range(0, 8))
    op(POOL, "memset", dict(ap=krT[64:128, :], constant=0.0), reads=["w_in"], writes=["krT"])
    xload(range(4, 16))
    hstats(0, list(range(0, 4)))

    def p1_a(tt):
        sl = tt % 2
        op(DVE, "tensor_scalar", dict(out=u_bf[sl], in0=h[:, tt, :], scalar1=st[:, RSTDH + tt:RSTDH + tt + 1],
                                      scalar2=None, op0=ALU.mult),
           reads=["h%d" % tt, "rstdh%d" % (tt // 4)], writes=["u_bf%d" % sl])
        for c in range(8):
            op(PE, "transpose", dict(out=psb(0)[:, c * 128:(c + 1) * 128], in_=u_bf[sl][:, c * 128:(c + 1) * 128],
                                     identity=ident[:]),
               reads=["u_bf%d" % sl, "ident"], writes=[PSB[0]])
        evac(uTs[sl][:].rearrange("p c t -> p (c t)"), psb(0)[:, 0:1024], [PSB[0]], ["uTs%d" % sl], eng=ACT)

    def p1_b(tt):
        sl = tt % 2
        blocks = [(1, 0, 0, 512), (2, 0, 512, 768), (2, 256, 1280, 1344), (3, 0, 768, 1280),
                  (5, 0, 1344, 1856), (6, 0, 1856, 2368)]
        for (bk, po, c0, c1) in blocks:
            for k in range(8):
                op(PE, "matmul", dict(out=psf(bk)[:, po:po + c1 - c0], lhsT=uTs[sl][:, k, :],
                                      rhs=w_in[:, k, c0:c1], start=(k == 0), stop=(k == 7)),
                   reads=["uTs%d" % sl, "w_in"], writes=[PSB[bk]])
        cq_ps = PSf[:, 512:512 + 768]
        ckv_ps = psf(3)
        op(ACT, "activation", dict(out=junk1[:, 0:768], in_=cq_ps, func=AF.Square, accum_out=st[:, SSQ:SSQ + 1]),
           reads=[PSB[1], PSB[2]], writes=["junk1", "ssq"])
        op(ACT, "activation", dict(out=junk1[:, 0:512], in_=ckv_ps, func=AF.Square, accum_out=st[:, SSQ + 1:SSQ + 2]),
           reads=[PSB[3]], writes=["junk1", "sskv"])
        rstd_calc(st[:, SSQ:SSQ + 1], 768, st[:, RSQ:RSQ + 1], st[:, LNT:LNT + 1], ["ssq"], ["rsq"], "lnt")
        rstd_calc(st[:, SSQ + 1:SSQ + 2], 512, st[:, RSQ + 1:RSQ + 2], st[:, LNT + 1:LNT + 2], ["sskv"], ["rskv"], "lnt1")
        op(DVE, "tensor_scalar", dict(out=cq_bf, in0=cq_ps, scalar1=st[:, RSQ:RSQ + 1], scalar2=None, op0=ALU.mult),
           reads=[PSB[1], PSB[2], "rsq"], writes=["cq_bf"])
        op(DVE, "tensor_scalar", dict(out=ckv_bf, in0=ckv_ps, scalar1=st[:, RSQ + 1:RSQ + 2], scalar2=None, op0=ALU.mult),
           reads=[PSB[3], "rskv"], writes=["ckv_bf"])
        kr_ps = psf(2)[:, 256:320]
        t1, t2, t3, t4 = rtmp[:, 0:32], rtmp[:, 32:64], rtmp[:, 64:96], rtmp[:, 96:128]
        cs, sn = cosA[:, tt, :], sinA[:, tt, :]
        op(DVE, "tensor_tensor", dict(out=t1, in0=kr_ps[:, 0:32], in1=cs, op=ALU.mult), reads=[PSB[2], "cosA"], writes=["rt0"])
        op(DVE, "tensor_tensor", dict(out=t2, in0=kr_ps[:, 32:64], in1=sn, op=ALU.mult), reads=[PSB[2], "sinA"], writes=["rt1"])
        op(DVE, "tensor_tensor", dict(out=kr_bf[:, 0:32], in0=t1, in1=t2, op=ALU.subtract), reads=["rt0", "rt1"], writes=["kr_bf"])
        op(DVE, "tensor_tensor", dict(out=t3, in0=kr_ps[:, 32:64], in1=cs, op=ALU.mult), reads=[PSB[2], "cosA"], writes=["rt2"])
        op(DVE, "tensor_tensor", dict(out=t4, in0=kr_ps[:, 0:32], in1=sn, op=ALU.mult), reads=[PSB[2], "sinA"], writes=["rt3"])
        op(DVE, "tensor_tensor", dict(out=kr_bf[:, 32:64], in0=t3, in1=t4, op=ALU.add), reads=["rt2", "rt3"], writes=["kr_bf"])
        zs = zstage[sl]
        op(ACT, "activation", dict(out=zs, in_=PSf[:, 5 * 512:7 * 512], func=AF.Silu),
           reads=[PSB[5], PSB[6]], writes=["zstage%d" % sl])
        P.dma(zscr[tt * 128:(tt + 1) * 128, :], zs, reads=["zstage%d" % sl], writes=["zscr%d" % tt], key="zst%d" % sl)

    def p1_c(tt):
        for c in range(6):
            op(PE, "transpose", dict(out=psb(7)[:, c * 128:(c + 1) * 128], in_=cq_bf[:, c * 128:(c + 1) * 128],
                                     identity=ident[:]),
               reads=["cq_bf", "ident"], writes=[PSB[7]])
        evac(cqnT[:, :, tt * 128:(tt + 1) * 128], psb(7)[:, 0:768].rearrange("p (c t) -> p c t", c=6), [PSB[7]], ["cqnT"], eng=DVE)
        for c in range(4):
            op(PE, "transpose", dict(out=psb(4)[:, c * 128:(c + 1) * 128], in_=ckv_bf[:, c * 128:(c + 1) * 128],
                                     identity=ident[:]),
               reads=["ckv_bf", "ident"], writes=[PSB[4]])
        op(PE, "transpose", dict(out=psb(4)[0:64, 512:640], in_=kr_bf, identity=ident[:]),
           reads=["kr_bf", "ident"], writes=[PSB[4]])
        evac(ckvnT[:, :, tt * 128:(tt + 1) * 128], psb(4)[:, 0:512].rearrange("p (c t) -> p c t", c=4), [PSB[4]], ["ckvnT"], eng=ACT)
        evac(krT[0:64, tt * 128:(tt + 1) * 128], psb(4)[0:64, 512:640], [PSB[4]], ["krT"], eng=DVE)

    p1_a(0)
    for tt in range(NT):
        if tt % 4 == 1 and tt < 12:
            hstats(0, list(range(tt + 3, tt + 7)))
        p1_b(tt)
        if tt + 1 < NT:
            p1_a(tt + 1)
        p1_c(tt)

    if debug:
        dB = nc.dram_tensor("dbgB", [128, 19456], BF16, kind="ExternalOutput").ap()
        P.dma(dB[:, :], arB[:], reads=list(P.bufs.values()), writes=["w_in"], key="dbg")
        dC1 = nc.dram_tensor("dbgC1", [128, 16384], BF16, kind="ExternalOutput").ap()
        P.dma(dC1[:, :], arC[:], reads=list(P.bufs.values()), writes=["cq_bf", "ckv_bf", "u_bf0", "u_bf1", "uTs0", "uTs1"], key="dbg")
    uq_d = a_w_uq.rearrange("(k p) n -> p k n", p=128)
    uk_d = a_w_uk.rearrange("(k p) n -> p k n", p=128)
    uv_d = a_w_uv.rearrange("(k p) n -> p k n", p=128)

    def l0_head_weights(hh):
        load_w(wq, uq_d[:, :, hh * 192:(hh + 1) * 192], [128, 6, 192],
               gn[:, 8:14].unsqueeze(2).to_broadcast([128, 6, 192]), "wq")
        load_w(wk, uk_d[:, :, hh * 128:(hh + 1) * 128], [128, 4, 128],
               gn[:, 14:18].unsqueeze(2).to_broadcast([128, 4, 128]), "wk")
        load_w(wv, uv_d[:, :, hh * 128:(hh + 1) * 128], [128, 4, 128],
               gn[:, 14:18].unsqueeze(2).to_broadcast([128, 4, 128]), "wv")

    zs_d = zscr.rearrange("(t p) (g c) -> p t g c", p=128, c=128)

    pe_fence()
    op(POOL, "memset", dict(ap=V[:, :, 128:129], constant=1.0), reads=fence["cur"], writes=["V"])
    op(POOL, "memset", dict(ap=QrT[64:128, :], constant=0.0), reads=fence["cur"], writes=["QrT"])
    l0_head_weights(0)
    for hh in range(nheads):
        P.dma(zbuf, zs_d[:, :, hh, :], reads=["zscr%d" % t for t in range(NT)] + fence["cur"], writes=["zbuf"], key="zl")
        for tb in range(4):
            bk = nbank()
            for k in range(4):
                op(PE, "matmul", dict(out=psf(bk), lhsT=wk[:, k, :], rhs=ckvnT[:, k, tb * 512:(tb + 1) * 512],
                                               start=(k == 0), stop=(k == 3)),
                   reads=["wk", "ckvnT"], writes=[PSB[bk]])
            evac(KT[:, tb * 512:(tb + 1) * 512], psf(bk), [PSB[bk]], ["KT"])
        for tb in range(4):
            bk = nbank()
            for k in range(6):
                op(PE, "matmul", dict(out=psf(bk), lhsT=wq[:, k, 0:128], rhs=cqnT[:, k, tb * 512:(tb + 1) * 512],
                                               start=(k == 0), stop=(k == 5)),
                   reads=["wq", "cqnT"], writes=[PSB[bk]])
            evac(QT[:, tb * 512:(tb + 1) * 512], psf(bk), [PSB[bk]], ["QT"])
        for j in range(4):
            bk = nbank()
            for i in range(4):
                tt = 4 * j + i
                for k in range(4):
                    op(PE, "matmul", dict(out=psf(bk)[:, i * 128:(i + 1) * 128],
                                                               lhsT=ckvnT[:, k, tt * 128:(tt + 1) * 128], rhs=wv[:, k, :],
                                                               start=(k == 0), stop=(k == 3)),
                       reads=["wv", "ckvnT"], writes=[PSB[bk]])
            evac(V[:, 4 * j:4 * j + 4, 0:128], psf(bk).rearrange("p (i c) -> p i c", i=4), [PSB[bk]], ["V"])
        def qr_p(j):
            bk = (6, 7)[j % 2]
            for i in range(4):
                tt = 4 * j + i
                for k in range(6):
                    op(PE, "matmul", dict(out=psf(bk)[:, i * 64:(i + 1) * 64],
                                          lhsT=cqnT[:, k, tt * 128:(tt + 1) * 128], rhs=wq[:, k, 128:192],
                                          start=(k == 0), stop=(k == 5)),
                       reads=["wq", "cqnT"], writes=[PSB[bk]])
            xq = psf(bk)[:, 0:256].rearrange("p (i d) -> p i d", i=4)
            cs, sn = cosA[:, 4 * j:4 * j + 4, :], sinA[:, 4 * j:4 * j + 4, :]
            t1 = rtmp[:, 0:128].rearrange("p (i d) -> p i d", i=4)
            t2 = rtmp[:, 128:256].rearrange("p (i d) -> p i d", i=4)
            o1f = o1n[:].rearrange("p a b -> p (a b)")
            t3 = o1f[:, 0:128].rearrange("p (i d) -> p i d", i=4)
            t4 = o1f[:, 128:256].rearrange("p (i d) -> p i d", i=4)
            qo = kqb[:, j % 2, 0:256].rearrange("p (i d) -> p i d", i=4)
            kt_ = "kqb%d" % (j % 2)
            op(DVE, "tensor_tensor", dict(out=t1, in0=xq[:, :, 0:32], in1=cs, op=ALU.mult), reads=[PSB[bk], "cosA"], writes=["rt0"])
            op(DVE, "tensor_tensor", dict(out=t2, in0=xq[:, :, 32:64], in1=sn, op=ALU.mult), reads=[PSB[bk], "sinA"], writes=["rt1"])
            op(DVE, "tensor_tensor", dict(out=qo[:, :, 0:32], in0=t1, in1=t2, op=ALU.subtract), reads=["rt0", "rt1"], writes=[kt_])
            op(DVE, "tensor_tensor", dict(out=t3, in0=xq[:, :, 32:64], in1=cs, op=ALU.mult), reads=[PSB[bk], "cosA"], writes=["rt2"])
            op(DVE, "tensor_tensor", dict(out=t4, in0=xq[:, :, 0:32], in1=sn, op=ALU.mult), reads=[PSB[bk], "sinA"], writes=["rt3"])
            op(DVE, "tensor_tensor", dict(out=qo[:, :, 32:64], in0=t3, in1=t4, op=ALU.add), reads=["rt2", "rt3"], writes=[kt_])

        def qr_t(j):
            bt = (4, 5)[j % 2]
            kt_ = "kqb%d" % (j % 2)
            for i in range(4):
                op(PE, "transpose", dict(out=psb(bt)[0:64, i * 128:(i + 1) * 128], in_=kqb[:, j % 2, i * 64:(i + 1) * 64],
                                         identity=ident[:]),
                   reads=[kt_, "ident"], writes=[PSB[bt]])
            evac(QrT[0:64, j * 512:(j + 1) * 512], psb(bt)[0:64, 0:512], [PSB[bt]], ["QrT"])

        for j in range(4):
            qr_p(j)
            if j >= 1:
                qr_t(j - 1)
        qr_t(3)
        if hh + 1 < nheads:
            l0_head_weights(hh + 1)
        else:
            load_wo(a_w_o, gpost_a, ["cqnT", "ckvnT"])
        attention(hh, 0, 1)

    pe_fence()
    out_proj(0, a_w_o, gpost_a, last=(n_layers == 1))
    pe_fence()

    if n_layers == 2:
        def l1_u(tt):
            sl = tt % 2
            if tt % 2 == 0:
                op(DVE, "tensor_scalar", dict(out=u_bf[sl], in0=h[:, tt, :], scalar1=st[:, RSTDH + tt:RSTDH + tt + 1],
                                              scalar2=None, op0=ALU.mult),
                   reads=["h%d" % tt, "rstdh%d" % (tt // 4)], writes=["u_bf%d" % sl])
            else:
                op(ACT, "activation", dict(out=u_bf[sl], in_=h[:, tt, :], func=AF.Copy,
                                           scale=st[:, RSTDH + tt:RSTDH + tt + 1]),
                   reads=["h%d" % tt, "rstdh%d" % (tt // 4)], writes=["u_bf%d" % sl])

        l1_u(0)
        for tt in range(NT):
            sl = tt % 2
            if tt + 1 < NT:
                l1_u(tt + 1)
            bk = nbank()
            for c in range(8):
                op(PE, "transpose", dict(out=psb(bk)[:, c * 128:(c + 1) * 128], in_=u_bf[sl][:, c * 128:(c + 1) * 128],
                                         identity=ident[:]),
                   reads=["u_bf%d" % sl, "ident"], writes=[PSB[bk]])
            evac(uT[:, :, tt * 128:(tt + 1) * 128], psb(bk)[:, 0:1024].rearrange("p (c t) -> p c t", c=8), [PSB[bk]], ["uT"],
                 eng=(ACT if tt % 2 == 0 else DVE))

        kv_d = b_w_kv.rearrange("(k p) n -> p k n", p=128)
        in_d = b_w_in.rearrange("(k p) n -> p k n", p=128)
        g_kv = gn[:, 18:26].unsqueeze(2).to_broadcast([128, 8, 128])
        g_pre = gn[:, 26:34].unsqueeze(2).to_broadcast([128, 8, 128])

        def l1_head_weights(hh):
            load_w(wg[:, :, 0:128], kv_d[:, :, hh * 128:(hh + 1) * 128], [128, 8, 128], g_kv, "wg")
            load_w(wg[:, :, 128:256], in_d[:, :, hh * 128:(hh + 1) * 128], [128, 8, 128], g_pre, "wg")
            load_w(wg[:, :, 256:384], kv_d[:, :, 1024 + hh * 128:1024 + (hh + 1) * 128], [128, 8, 128], g_kv, "wg")
            load_w(wg[:, :, 384:512], in_d[:, :, 1024 + hh * 128:1024 + (hh + 1) * 128], [128, 8, 128], g_pre, "wg")

        op(POOL, "memset", dict(ap=V[:, :, 128:129], constant=1.0), reads=fence["cur"], writes=["V"])
        op(POOL, "memset", dict(ap=KT[64:128, :], constant=0.0), reads=fence["cur"], writes=["KT"])
        op(POOL, "memset", dict(ap=QrT[0:64, :], constant=0.0), reads=fence["cur"], writes=["QrT"])
        l1_head_weights(0)
        def l1_proj_p(j):
            pa = (0, 2, 6)[j % 3]
            for i in range(2):
                tt = 2 * j + i
                for (bk_, c0_) in ((pa, 0), (pa + 1, 256)):
                    for k in range(8):
                        op(PE, "matmul", dict(out=psf(bk_)[:, i * 256:(i + 1) * 256], lhsT=uT[:, k, tt * 128:(tt + 1) * 128],
                                              rhs=wg[:, k, c0_:c0_ + 256], start=(k == 0), stop=(k == 7)),
                           reads=["uT", "wg"], writes=[PSB[bk_]])
            return pa

        def l1_proj_r(j, pa):
            tt0 = 2 * j
            s0 = 2 * (j % 2)
            xa = psf(pa).rearrange("p (i c d) -> p i c d", i=2, c=4)
            kq4 = kqb[:, s0:s0 + 2, :].rearrange("p i (c d) -> p i c d", c=4)
            rest, rk = "kqb_rest%d" % (j % 2), "kqb_rope%d" % (j % 2)
            op(ACT, "copy", dict(out=kq4[:, :, :, 16:64], in_=xa[:, :, :, 16:64]), reads=[PSB[pa]], writes=[rest])
            cs = cosB[:, tt0:tt0 + 2, :].unsqueeze(2).to_broadcast([128, 2, 4, 8])
            sn = sinB[:, tt0:tt0 + 2, :].unsqueeze(2).to_broadcast([128, 2, 4, 8])
            t1, t2, t3, t4 = [rtmp[:, q_ * 64:(q_ + 1) * 64].rearrange("p (i c d) -> p i c d", i=2, c=4) for q_ in range(4)]
            op(DVE, "tensor_tensor", dict(out=t1, in0=xa[:, :, :, 0:8], in1=cs, op=ALU.mult), reads=[PSB[pa], "cosB"], writes=["rt0"])
            op(DVE, "tensor_tensor", dict(out=t2, in0=xa[:, :, :, 8:16], in1=sn, op=ALU.mult), reads=[PSB[pa], "sinB"], writes=["rt1"])
            op(DVE, "tensor_tensor", dict(out=kq4[:, :, :, 0:8], in0=t1, in1=t2, op=ALU.subtract), reads=["rt0", "rt1"], writes=[rk])
            op(DVE, "tensor_tensor", dict(out=t3, in0=xa[:, :, :, 8:16], in1=cs, op=ALU.mult), reads=[PSB[pa], "cosB"], writes=["rt2"])
            op(DVE, "tensor_tensor", dict(out=t4, in0=xa[:, :, :, 0:8], in1=sn, op=ALU.mult), reads=[PSB[pa], "sinB"], writes=["rt3"])
            op(DVE, "tensor_tensor", dict(out=kq4[:, :, :, 8:16], in0=t3, in1=t4, op=ALU.add), reads=["rt2", "rt3"], writes=[rk])
            xb = psf(pa + 1).rearrange("p (i c) -> p i c", i=2)
            op(DVE, "tensor_copy", dict(out=V[:, tt0:tt0 + 2, 0:128], in_=xb[:, :, 0:128]),
               reads=[PSB[pa + 1]] + fence["cur"], writes=["V"])
            op(ACT, "activation", dict(out=zbuf[:, tt0:tt0 + 2, :], in_=xb[:, :, 128:256], func=AF.Silu),
               reads=[PSB[pa + 1]] + fence["cur"], writes=["zbuf"])
            op(POOL, "tensor_tensor", dict(out=zbuf[:, tt0:tt0 + 2, :], in0=zbuf[:, tt0:tt0 + 2, :],
                                           in1=g2b[:].unsqueeze(1).to_broadcast([128, 2, 128]), op=ALU.mult),
               reads=["zbuf", "g2b"], writes=["zbuf"])

        def l1_proj_t(j):
            bt = 4 + j % 2
            s0 = 2 * (j % 2)
            rest, rk = "kqb_rest%d" % (j % 2), "kqb_rope%d" % (j % 2)
            for i in range(2):
                for s_ in range(2):
                    op(PE, "transpose", dict(out=psb(bt)[:, (i * 2 + s_) * 128:(i * 2 + s_ + 1) * 128],
                                             in_=kqb[:, s0 + i, s_ * 128:(s_ + 1) * 128], identity=ident[:]),
                       reads=[rest, rk, "ident"], writes=[PSB[bt]])
            pv = psb(bt)[:, 0:512].rearrange("p (i s c) -> p i s c", i=2, s=2)
            evac(KT[0:64, j * 256:(j + 1) * 256].rearrange("p (i c) -> p i c", i=2), pv[0:64, :, 0, :], [PSB[bt]], ["KT"])
            evac(QrT[64:128, j * 256:(j + 1) * 256].rearrange("p (i c) -> p i c", i=2), pv[64:128, :, 0, :], [PSB[bt]], ["QrT"])
            evac(QT[:, j * 256:(j + 1) * 256].rearrange("p (i c) -> p i c", i=2), pv[:, :, 1, :], [PSB[bt]], ["QT"])

        for hh in range(8):
            for j in range(8):
                pa_ = l1_proj_p(j)
                l1_proj_r(j, pa_)
                if j >= 1:
                    l1_proj_t(j - 1)
            l1_proj_t(7)
            if hh + 1 < 8:
                l1_head_weights(hh + 1)
            else:
                load_wo(b_w_o, gpost_b, ["uT"])
            attention(hh, 1, 2)
        pe_fence()
        out_proj(1, b_w_o, gpost_b, last=True)

    fk = ["out"]
    if debug:
        dA = nc.dram_tensor("dbgA", [128, 22528], BF16, kind="ExternalOutput").ap()
        dC = nc.dram_tensor("dbgC", [128, 16384], BF16, kind="ExternalOutput").ap()
        dS = nc.dram_tensor("dbgS", [128, 64], F32, kind="ExternalOutput").ap()
        dT = nc.dram_tensor("dbgT", [128, 4 * NT * 32], F32, kind="ExternalOutput").ap()
        allb = list(P.bufs.values())
        P.dma(dA[:, :], arA[:], reads=allb, key="dbg")
        P.dma(dC[:, :], arC[:], reads=allb, key="dbg")
        P.dma(dS[:, :], st[:], reads=allb, key="dbg")
        P.dma(dT[:, 0:512], cosA[:].rearrange("p t f -> p (t f)"), reads=allb, key="dbg")
        P.dma(dT[:, 512:1024], sinA[:].rearrange("p t f -> p (t f)"), reads=allb, key="dbg")
        P.dma(dT[:, 1024:1152], cosB[:].rearrange("p t f -> p (t f)"), reads=allb, key="dbg")
        P.dma(dT[:, 1152:1280], sinB[:].rearrange("p t f -> p (t f)"), reads=allb, key="dbg")
        fk.append("dbg")
    P.emit(final_wait_keys=fk)
    nc._op_counts = {e: sum(1 for o in P.ops if o.eng == e) for e in ENGINES}
    return nc


_CACHE = {}


def _prep_inputs(inputs):
    f = lambda a: np.ascontiguousarray(np.asarray(a), dtype=np.float32)
    x = f(inputs["x"])
    pos = np.asarray(inputs["positions"]).astype(np.int32)

    def chunked(g):
        g = f(g).reshape(-1, 128)
        return np.ascontiguousarray(g.T)

    gains = np.concatenate([chunked(inputs["a_pre_g"][0]), chunked(inputs["a_q_norm_g"][0]),
                            chunked(inputs["a_kv_norm_g"][0]), chunked(inputs["b_kv_norm_g"]),
                            chunked(inputs["b_pre_g"][0])], axis=1)
    rep = lambda v, n=128: np.ascontiguousarray(np.broadcast_to(f(v).reshape(1, -1), (n, f(v).size)))
    invA = (10000.0 ** (-np.arange(0, 64, 2, dtype=np.float32) / np.float32(64))).astype(np.float32)
    invB = (500000.0 ** (-np.arange(0, 16, 2, dtype=np.float32) / np.float32(16))).astype(np.float32)
    cst = rep(np.concatenate([invA, invB]))
    shared = {
        "a_w_in": f(inputs["a_w_in"][0]), "a_w_uq": f(inputs["a_w_uq"][0]), "a_w_uk": f(inputs["a_w_uk"][0]),
        "a_w_uv": f(inputs["a_w_uv"][0]), "a_w_o": f(inputs["a_w_o"][0]), "b_w_kv": f(inputs["b_w_kv"]),
        "b_w_in": f(inputs["b_w_in"][0]), "b_w_o": f(inputs["b_w_o"][0]),
        "gains": gains, "gpost_a": rep(inputs["a_post_g"][0]), "gpost_b": rep(inputs["b_post_g"][0]),
        "lam_in": rep(inputs["b_lambda"][0]), "subln_in": rep(inputs["b_subln_g"][0]),
        "cst": cst, "ident_in": np.eye(128, dtype=np.float32),
    }
    maps = []
    for b in range(8):
        m = dict(shared)
        m["x"] = x[b]
        m["pos"] = np.ascontiguousarray(pos[b].reshape(NT, 128).T)
        maps.append(m)
    return maps


def kernel(**inputs):
    if "nc" not in _CACHE:
        _CACHE["nc"] = build(2)
    maps = _prep_inputs(inputs)
    res = run_bass_kernel_spmd(_CACHE["nc"], maps, core_ids=list(range(8)))
    return np.stack([np.asarray(res.results[i]["out"], dtype=np.float32) for i in range(8)], axis=0)
```
